# Optimizing a Trainium2 kernel written in Bass

```python
import math, functools
import jax, jax.numpy as jnp
from jax import lax
import numpy as np

D_MODEL = 1024
BATCH = 16
SEQ = 256
DEPTH = 2
DEC_BATCH = 8
DEC_SEQ = 2048
PAST_LEN = 512

GRID_W = 64
HEAD_DIM = 64
AXIS_DIM = HEAD_DIM // 2
A_HEADS = 4
A_VDIM = 2 * HEAD_DIM
CONV_DIM = 512
CONV_WIDTH = 3
C_HEADS = 16
C_KV_HEADS = 4
WINDOW = 128
Q_BLOCK = 128
D_FF = 2816
ROPE_BASE = 10000.0
N_MOD = 9
EPS = 1e-6
N_EVEN = (DEPTH + 1) // 2
N_ODD = DEPTH // 2
A_QK = A_HEADS * 2 * HEAD_DIM
A_V = A_HEADS * A_VDIM
EVEN_SPLITS = [A_QK, 2 * A_QK, 2 * A_QK + A_V, 2 * A_QK + A_V + CONV_DIM, 2 * A_QK + A_V + 2 * CONV_DIM]
EVEN_IN = 2 * A_QK + A_V + 3 * CONV_DIM
EVEN_OUT = A_V + CONV_DIM
ODD_IN = (C_HEADS + 2 * C_KV_HEADS) * HEAD_DIM
ODD_OUT = C_HEADS * HEAD_DIM

kernel_name = "hybrid_diff_conv_swa_prefix_dit_step"


def _rms_norm(x, g):
    xf = x.astype(jnp.float32)
    y = xf * lax.rsqrt(jnp.mean(xf * xf, axis=-1, keepdims=True) + EPS)
    return (y * g.astype(jnp.float32)).astype(x.dtype)


def _adaln(cvec, w, b):
    m = jax.nn.silu(cvec) @ w + b
    return m.reshape(cvec.shape[0], N_MOD, D_MODEL)


def _swiglu(h, w13, w2):
    gate, up = jnp.split(h @ w13, 2, axis=-1)
    return (jax.nn.silu(gate) * up) @ w2


def _axial_rope(n):
    rows = n // GRID_W
    row = jnp.repeat(jnp.arange(rows, dtype=jnp.float32), GRID_W)
    col = jnp.tile(jnp.arange(GRID_W, dtype=jnp.float32), rows)
    inv = ROPE_BASE ** (-jnp.arange(0, AXIS_DIM, 2, dtype=jnp.float32) / AXIS_DIM)
    ang_r = row[:, None] * inv[None, :]
    ang_c = col[:, None] * inv[None, :]
    ang = jnp.concatenate([ang_r, ang_r, ang_c, ang_c], axis=-1)
    return jnp.cos(ang), jnp.sin(ang)


def _rope(x, cos, sin):
    shape = (1, x.shape[1]) + (1,) * (x.ndim - 3) + (HEAD_DIM,)
    cos = cos.reshape(shape).astype(x.dtype)
    sin = sin.reshape(shape).astype(x.dtype)
    x1, x2, x3, x4 = jnp.split(x, 4, axis=-1)
    rot = jnp.concatenate([-x2, x1, -x4, x3], axis=-1)
    return x * cos + rot * sin


def _short_conv(u, w):
    up = jnp.pad(u, ((0, 0), (1, 1), (0, 0)))
    return w[0] * up[:, :-2] + w[1] * up[:, 1:-1] + w[2] * up[:, 2:]


def _query_blocks(q):
    b, s = q.shape[:2]
    nb = s // Q_BLOCK
    return jnp.moveaxis(q.reshape((b, nb, Q_BLOCK) + q.shape[2:]), 1, 0)


def _merge_blocks(o):
    nb, b = o.shape[:2]
    return jnp.moveaxis(o, 0, 1).reshape((b, nb * Q_BLOCK) + o.shape[3:])


def _diff_attention(q, k, v, lam):
    scale = HEAD_DIM ** -0.5

    def block(qi):
        s = jnp.einsum("bqhmd,bkhmd->bhmqk", qi, k, preferred_element_type=jnp.float32) * scale
        p = jax.nn.softmax(s, axis=-1)
        a = p[:, :, 0] - lam * p[:, :, 1]
        return jnp.einsum("bhqk,bkhe->bqhe", a.astype(v.dtype), v, preferred_element_type=jnp.float32).astype(v.dtype)

    return _merge_blocks(lax.map(block, _query_blocks(q)))


def _sink_column(sink, s):
    col = sink.astype(jnp.float32).reshape(1, C_KV_HEADS, -1, 1, 1)
    return jnp.broadcast_to(col, s.shape[:-1] + (1,))


def _dense_sink_attention(q, k, v, sink):
    b, s_len, h, dh = q.shape
    qg = q.reshape(b, s_len, C_KV_HEADS, h // C_KV_HEADS, dh)
    scale = HEAD_DIM ** -0.5

    def block(qi):
        s = jnp.einsum("bqgrd,btgd->bgrqt", qi, k, preferred_element_type=jnp.float32) * scale
        p = jax.nn.softmax(jnp.concatenate([s, _sink_column(sink, s)], axis=-1), axis=-1)[..., :-1]
        return jnp.einsum("bgrqt,btgd->bqgrd", p.astype(v.dtype), v, preferred_element_type=jnp.float32).astype(v.dtype)

    return _merge_blocks(lax.map(block, _query_blocks(qg))).reshape(b, s_len, h, dh)


def _window_sink_attention(q, k, v, k_ctx, v_ctx, sink):
    b, n, h, dh = q.shape
    p_len = k_ctx.shape[1]
    nb = n // Q_BLOCK
    qg = q.reshape(b, n, C_KV_HEADS, h // C_KV_HEADS, dh)
    pad = ((0, 0), (Q_BLOCK, Q_BLOCK), (0, 0), (0, 0))
    k_pad = jnp.pad(k, pad)
    v_pad = jnp.pad(v, pad)
    offs_q = jnp.arange(Q_BLOCK)
    offs_k = jnp.arange(3 * Q_BLOCK) - Q_BLOCK
    scale = HEAD_DIM ** -0.5

    def block(args):
        i, qi = args
        start = i * Q_BLOCK
        ks = lax.dynamic_slice_in_dim(k_pad, start, 3 * Q_BLOCK, axis=1)
        vs = lax.dynamic_slice_in_dim(v_pad, start, 3 * Q_BLOCK, axis=1)
        qpos = start + offs_q
        kpos = start + offs_k
        valid = (jnp.abs(qpos[:, None] - kpos[None, :]) <= WINDOW) & (kpos[None, :] >= 0) & (kpos[None, :] < n)
        s_c = jnp.einsum("bqgrd,bcgd->bgrqc", qi, k_ctx, preferred_element_type=jnp.float32) * scale
        s_l = jnp.einsum("bqgrd,bkgd->bgrqk", qi, ks, preferred_element_type=jnp.float32) * scale
        s_l = jnp.where(valid, s_l, -jnp.inf)
        p = jax.nn.softmax(jnp.concatenate([s_c, s_l, _sink_column(sink, s_c)], axis=-1), axis=-1)
        o = (jnp.einsum("bgrqc,bcgd->bqgrd", p[..., :p_len].astype(v.dtype), v_ctx, preferred_element_type=jnp.float32)
             + jnp.einsum("bgrqk,bkgd->bqgrd", p[..., p_len:-1].astype(v.dtype), vs, preferred_element_type=jnp.float32))
        return o.astype(v.dtype)

    o = lax.map(block, (jnp.arange(nb), _query_blocks(qg)))
    return _merge_blocks(o).reshape(b, n, h, dh)


def _even_mixer(h, w_in, w_out, qk_g, lam_vec, subln_g, conv_w, lam_init, rope=None, kv_ctx=None):
    b, s, _ = h.shape
    q, k, v, bg, cg, xin = jnp.split(h @ w_in, EVEN_SPLITS, axis=-1)
    q = _rms_norm(q.reshape(b, s, A_HEADS, 2, HEAD_DIM), qk_g[0])
    k = _rms_norm(k.reshape(b, s, A_HEADS, 2, HEAD_DIM), qk_g[1])
    v = v.reshape(b, s, A_HEADS, A_VDIM)
    lf = lam_vec.astype(jnp.float32)
    lam = jnp.exp(jnp.sum(lf[0] * lf[1])) - jnp.exp(jnp.sum(lf[2] * lf[3])) + lam_init
    if kv_ctx is None:
        aux = (k.reshape(b, s, A_HEADS, 2 * HEAD_DIM), v)
        o = _diff_attention(q, k, v, lam)
    else:
        cos, sin = rope
        k_ctx, v_ctx = kv_ctx
        k_all = jnp.concatenate([k_ctx.reshape(b, -1, A_HEADS, 2, HEAD_DIM).astype(k.dtype), _rope(k, cos, sin)], axis=1)
        v_all = jnp.concatenate([v_ctx.astype(v.dtype), v], axis=1)
        o = _diff_attention(_rope(q, cos, sin), k_all, v_all, lam)
        aux = None
    o = _rms_norm(o, subln_g) * (1.0 - lam_init)
    y = bg * _short_conv(cg * xin, conv_w)
    out = jnp.concatenate([o.reshape(b, s, A_V), y], axis=-1) @ w_out
    return out, aux


def _odd_mixer(h, w_in, w_out, qk_g, sink, rope=None, kv_ctx=None):
    b, s, _ = h.shape
    q, k, v = jnp.split(h @ w_in, [C_HEADS * HEAD_DIM, (C_HEADS + C_KV_HEADS) * HEAD_DIM], axis=-1)
    q = _rms_norm(q.reshape(b, s, C_HEADS, HEAD_DIM), qk_g[0])
    k = _rms_norm(k.reshape(b, s, C_KV_HEADS, HEAD_DIM), qk_g[1])
    v = v.reshape(b, s, C_KV_HEADS, HEAD_DIM)
    if kv_ctx is None:
        aux = (k, v)
        o = _dense_sink_attention(q, k, v, sink)
    else:
        cos, sin = rope
        k_ctx, v_ctx = kv_ctx
        o = _window_sink_attention(_rope(q, cos, sin), _rope(k, cos, sin), v,
                                   k_ctx.astype(k.dtype), v_ctx.astype(v.dtype), sink)
        aux = None
    return o.reshape(b, s, ODD_OUT) @ w_out, aux


def _layer(x, m, g, w13, w2, mixer):
    m = m[:, None]
    h = _rms_norm(x, g[0]) * (1 + m[:, :, 1]) + m[:, :, 0]
    x = x + 0.5 * m[:, :, 2] * _swiglu(h, w13[0], w2[0])
    h = _rms_norm(x, g[1]) * (1 + m[:, :, 4]) + m[:, :, 3]
    mix, aux = mixer(h)
    x = x + m[:, :, 5] * mix
    h = _rms_norm(x, g[2]) * (1 + m[:, :, 7]) + m[:, :, 6]
    x = x + 0.5 * m[:, :, 8] * _swiglu(h, w13[1], w2[1])
    return x, aux


def setup_inputs(seed: int = 0) -> dict:
    key = jax.random.key(seed)
    ks = jax.random.split(key, 24)

    def nrm(k, shape, s=1.0):
        return jax.random.normal(k, shape, jnp.float32) * s

    return {
        "x_prompt": nrm(ks[0], (BATCH, SEQ, D_MODEL)),
        "x_sample": nrm(ks[1], (DEC_BATCH, DEC_SEQ, D_MODEL)),
        "cache_even_k": nrm(ks[2], (DEC_BATCH, N_EVEN, PAST_LEN, A_HEADS, 2 * HEAD_DIM)),
        "cache_even_v": nrm(ks[3], (DEC_BATCH, N_EVEN, PAST_LEN, A_HEADS, A_VDIM)),
        "cache_odd_k": nrm(ks[4], (DEC_BATCH, N_ODD, PAST_LEN, C_KV_HEADS, HEAD_DIM)),
        "cache_odd_v": nrm(ks[5], (DEC_BATCH, N_ODD, PAST_LEN, C_KV_HEADS, HEAD_DIM)),
        "c": nrm(ks[6], (DEC_BATCH, D_MODEL)),
        "c_ctx": nrm(ks[7], (D_MODEL,)),
        "w_mod": nrm(ks[8], (DEPTH, D_MODEL, N_MOD * D_MODEL), 0.5 * D_MODEL ** -0.5),
        "b_mod": nrm(ks[9], (DEPTH, N_MOD * D_MODEL), 0.02),
        "norm_g": 1.0 + nrm(ks[10], (DEPTH, 3, D_MODEL), 0.02),
        "ffn_w13": nrm(ks[11], (DEPTH, 2, D_MODEL, 2 * D_FF), D_MODEL ** -0.5),
        "ffn_w2": nrm(ks[12], (DEPTH, 2, D_FF, D_MODEL), D_FF ** -0.5),
        "even_w_in": nrm(ks[13], (N_EVEN, D_MODEL, EVEN_IN), D_MODEL ** -0.5),
        "even_w_out": nrm(ks[14], (N_EVEN, EVEN_OUT, D_MODEL), EVEN_OUT ** -0.5),
        "even_qk_norm": 1.0 + nrm(ks[15], (N_EVEN, 2, HEAD_DIM), 0.02),
        "even_lambda": nrm(ks[16], (N_EVEN, 4, HEAD_DIM), 0.1),
        "even_subln": 1.0 + nrm(ks[17], (N_EVEN, A_VDIM), 0.02),
        "even_conv_w": nrm(ks[18], (N_EVEN, CONV_WIDTH, CONV_DIM), CONV_WIDTH ** -0.5),
        "odd_w_in": nrm(ks[19], (N_ODD, D_MODEL, ODD_IN), D_MODEL ** -0.5),
        "odd_w_out": nrm(ks[20], (N_ODD, ODD_OUT, D_MODEL), ODD_OUT ** -0.5),
        "odd_qk_norm": 1.0 + nrm(ks[21], (N_ODD, 2, HEAD_DIM), 0.02),
        "odd_sink": nrm(ks[22], (N_ODD, C_HEADS), 0.5),
    }


def reference(x_prompt, x_sample, cache_even_k, cache_even_v, cache_odd_k, cache_odd_v, c, c_ctx,
              w_mod, b_mod, norm_g, ffn_w13, ffn_w2,
              even_w_in, even_w_out, even_qk_norm, even_lambda, even_subln, even_conv_w,
              odd_w_in, odd_w_out, odd_qk_norm, odd_sink):
    rope = _axial_rope(x_sample.shape[1])
    yp, ys = x_prompt, x_sample
    even_k, even_v, odd_k, odd_v = [], [], [], []
    for l in range(DEPTH):
        m_ctx = _adaln(c_ctx[None, :], w_mod[l], b_mod[l])
        m_lat = _adaln(c, w_mod[l], b_mod[l])
        if l % 2 == 0:
            e = l // 2
            lam_init = 0.8 - 0.6 * math.exp(-0.3 * l)
            mix = functools.partial(_even_mixer, w_in=even_w_in[e], w_out=even_w_out[e], qk_g=even_qk_norm[e],
                                    lam_vec=even_lambda[e], subln_g=even_subln[e], conv_w=even_conv_w[e],
                                    lam_init=lam_init)
            yp, (kc, vc) = _layer(yp, m_ctx, norm_g[l], ffn_w13[l], ffn_w2[l], mix)
            ys, _ = _layer(ys, m_lat, norm_g[l], ffn_w13[l], ffn_w2[l],
                           functools.partial(mix, rope=rope, kv_ctx=(cache_even_k[:, e], cache_even_v[:, e])))
            even_k.append(kc)
            even_v.append(vc)
        else:
            o = l // 2
            mix = functools.partial(_odd_mixer, w_in=odd_w_in[o], w_out=odd_w_out[o], qk_g=odd_qk_norm[o],
                                    sink=odd_sink[o])
            yp, (kc, vc) = _layer(yp, m_ctx, norm_g[l], ffn_w13[l], ffn_w2[l], mix)
            ys, _ = _layer(ys, m_lat, norm_g[l], ffn_w13[l], ffn_w2[l],
                           functools.partial(mix, rope=rope, kv_ctx=(cache_odd_k[:, o], cache_odd_v[:, o])))
            odd_k.append(kc)
            odd_v.append(vc)
    return (yp, ys, jnp.stack(even_k, axis=1), jnp.stack(even_v, axis=1), jnp.stack(odd_k, axis=1), jnp.stack(odd_v, axis=1))
```

```python
import numpy as np
from contextlib import ExitStack
import concourse.bass as bass
import concourse.mybir as mybir
from concourse.bass_utils import run_bass_kernel_spmd

F32 = mybir.dt.float32
BF16 = mybir.dt.bfloat16
ALU = mybir.AluOpType
AF = mybir.ActivationFunctionType
AX = mybir.AxisListType

ENGS = ("pe", "act", "dve", "pool", "sp")

D = 1024
NT = 2560
TL = 512
NTILE = 5
NBLK = 20
DFF = 2816
NF = 22
G = 2
NFG = NF // G
EPS = 1e-6
SLOT = 6144
NF32 = 8
NBF = 8


class Sched:
    def __init__(self):
        self.streams = {e: [] for e in ENGS}
        self.cnt = {}
        self.res = {}
        self.seen = {e: {} for e in ENGS}
        self.sem_names = []

    def _sem(self, name):
        if name not in self.cnt:
            self.cnt[name] = 0
            self.sem_names.append(name)
        return name

    def op(self, eng, fn, reads=(), writes=(), dma=None, ndma=1):
        own = self._sem("E_" + eng)
        need = {}

        def want(evt, is_reader):
            if evt is None:
                return
            s, v = evt
            if s == own and dma is None:
                if eng == "pe":
                    return
            if need.get(s, 0) < v:
                need[s] = v

        for k in reads:
            r = self.res.get(k)
            if r is not None:
                want(r[0], False)
                if isinstance(k, tuple) and k[0] == "ps":
                    for s, v in r[1].items():
                        if s != own:
                            want((s, v), True)
        for k in writes:
            r = self.res.get(k)
            if r is not None:
                want(r[0], False)
                for s, v in r[1].items():
                    want((s, v), True)
        waits = []
        seen = self.seen[eng]
        for s, v in need.items():
            if seen.get(s, 0) < v:
                seen[s] = v
                waits.append((s, v))
        if dma is None:
            self.cnt[own] += 1
            evt = (own, self.cnt[own])
            self.streams[eng].append((waits, fn, ("c", own, 1)))
        else:
            self._sem(dma)
            self.cnt[dma] += 16 * ndma
            evt = (dma, self.cnt[dma])
            self.streams[eng].append((waits, fn, ("d", dma, ndma)))
        for k in reads:
            r = self.res.setdefault(k, [None, {}])
            if r[1].get(evt[0], 0) < evt[1]:
                r[1][evt[0]] = evt[1]
        for k in writes:
            self.res[k] = [evt, {}]
        return evt

    def emit(self, nc, st):
        H = {}
        for name in self.sem_names:
            H[name] = st.enter_context(nc.semaphore(name))
        block = st.enter_context(nc.Block())
        final = [(s, v) for s, v in self.cnt.items() if v > 0]

        def make(eng):
            stream = self.streams[eng]

            def body(e):
                for waits, fn, inc in stream:
                    for s, v in waits:
                        e.wait_ge(H[s], v)
                    r = fn(e)
                    if inc[0] == "c":
                        r.then_inc(H[inc[1]], 1)
                    else:
                        assert len(r) == inc[2], (len(r), inc[2])
                        for ins in r:
                            ins.then_inc(H[inc[1]], 16)
                if eng == "sp":
                    for s, v in final:
                        e.wait_ge(H[s], v)
            return body

        block.tensor(make("pe"))
        block.scalar(make("act"))
        block.vector(make("dve"))
        block.gpsimd(make("pool"))
        block.sync(make("sp"))


class TPool:
    def __init__(self, tiles, prefix):
        self.tiles = tiles
        self.prefix = prefix
        self.i = 0

    def get(self):
        n = len(self.tiles)
        j = self.i % n
        self.i += 1
        return self.tiles[j], (self.prefix, j)


SP_CT = 0
SP_BT = SP_CT + 16
SP_GT = SP_BT + 144
SP_CW = SP_GT + 48
SP_QKG = SP_CW + 12
SP_SUB = SP_QKG + 4
SP_LAM = SP_SUB + 1
SP_SINK = SP_LAM + 256
SP_SINKP = SP_SINK + 16
NSP = SP_SINKP + 8

CB_BONES = 0
CB_ONES = 128
CB_ROPE = 256
CB_IDENT = 384
CB_MASK = 512
NCB = CB_MASK + 6 * 512

CF_IDENT = 0
NCF = 128


def build_nc():
    nc = bass.Bass("TRN2", target_bir_lowering=False)
    S = Sched()

    def din(name, shape):
        return nc.dram_tensor(name, list(shape), F32, kind="ExternalInput").ap()

    def dout(name, shape):
        return nc.dram_tensor(name, list(shape), F32, kind="ExternalOutput").ap()

    xin = din("xin", [NT, D])
    cek = din("cek", [512, 512])
    cev = din("cev", [512, 512])
    cok = din("cok", [512, 256])
    cov = din("cov", [512, 256])
    spar = din("spar", [128, NSP])
    cb_d = din("cb", [128, NCB])
    cf_d = din("cf", [128, NCF])
    cs_d = din("cs", [128, 4096])
    wmod0 = din("wmod0", [18, 128, 4096])
    wmod1 = din("wmod1", [72, 128, 1024])
    w13 = [din(f"w13_{k}", [D, 2 * DFF]) for k in range(4)]
    w2 = [din(f"w2_{k}", [DFF, D]) for k in range(4)]
    ewin = din("ewin", [D, 3072])
    ewout = din("ewout", [D, D])
    owin = din("owin", [D, 1536])
    owout = din("owout", [D, D])

    ys = dout("ys", [2048, D])
    yp = dout("yp", [512, D])
    nek = dout("nek", [512, 512])
    nev = dout("nev", [512, 512])
    nok = dout("nok", [512, 256])
    nov = dout("nov", [512, 256])

    st = ExitStack()

    def sbt(name, shape, dt):
        return st.enter_context(nc.sbuf_tensor(name, list(shape), dt))

    xT = sbt("xT", [128, 8, NT], F32)
    hT = sbt("hT", [128, 8, NT], BF16)
    wslot = [sbt(f"wslot{i}", [128, SLOT], BF16) for i in range(2)]
    spt = sbt("spt", [128, NSP], F32)
    cbt = sbt("cbt", [128, NCB], BF16)
    cft = sbt("cft", [128, NCF], F32)
    mT = sbt("mT", [128, 2, 72, 2], F32)
    AB = sbt("AB", [128, 2, 9, 8, 2], F32)
    scT = sbt("scT", [128, 8, 2], BF16)
    misc = sbt("misc", [128, 64], F32)
    big = sbt("big", [128, 7680], BF16)
    kT = big[:, 0:3072]
    v3 = big[:, 3072:7680].rearrange("p (b f) -> p b f", b=24)
    vsb = v3[:, :, 0:128]
    ubuf = big[:, 0:5128].bitcast(F32)
    mslot = [big[:, 5632 + i * 1024: 5632 + (i + 1) * 1024] for i in range(2)]
    onp = TPool([sbt(f"on_{i}", [128, TL], BF16) for i in range(2)], "o")
    actb = [[sbt(f"act{p}_{j}", [128, TL], BF16) for j in range(G)] for p in range(2)]
    qtp = TPool([sbt(f"qT_{i}", [128, TL], BF16) for i in range(2)], "q")
    rsp = TPool([sbt(f"rs_{i}", [128, TL], F32) for i in range(1)], "r")
    f32p = TPool([sbt(f"f32_{i}", [128, TL], F32) for i in range(NF32)], "f")
    bfp = TPool([sbt(f"bf_{i}", [128, TL], BF16) for i in range(NBF)], "b")
    ps = [st.enter_context(nc.psum_tensor(f"ps{i}", [128, TL], F32)) for i in range(8)]

    def PK(i):
        return ("ps", i)

    bones = cbt[:, CB_BONES:CB_BONES + 128]
    ones = cbt[:, CB_ONES:CB_ONES + 128]
    ropeR = cbt[:, CB_ROPE:CB_ROPE + 128]
    identb = cbt[:, CB_IDENT:CB_IDENT + 128]
    identf = cft[:, CF_IDENT:CF_IDENT + 128]

    def maskap(mi, n=TL):
        return cbt[:, CB_MASK + mi * 512: CB_MASK + mi * 512 + n]


    def mm(out, pairs, reads, writes, start=True, stop=True):
        def fn(e, out=out, pairs=pairs, start=start, stop=stop):
            n = len(pairs)
            r = None
            for i, (l, rr) in enumerate(pairs):
                r = e.matmul(out, lhsT=l, rhs=rr, start=(start and i == 0), stop=(stop and i == n - 1))
            return r
        S.op("pe", fn, reads=reads, writes=writes)

    def tr(out, in_, ident, reads, writes):
        S.op("pe", lambda e, out=out, in_=in_, ident=ident: e.transpose(out, in_, ident),
             reads=reads, writes=writes)

    def act(out, in_, func, reads, writes, bias=None, scale=None):
        kw = {}
        if bias is not None:
            kw["bias"] = bias
        if scale is not None:
            kw["scale"] = scale
        S.op("act", lambda e, out=out, in_=in_, func=func, kw=kw: e.activation(out=out, in_=in_, func=func, **kw),
             reads=reads, writes=writes)

    def tt(eng, out, in0, in1, op, reads, writes):
        S.op(eng, lambda e, out=out, in0=in0, in1=in1, op=op: e.tensor_tensor(out=out, in0=in0, in1=in1, op=op),
             reads=reads, writes=writes)

    def stt(eng, out, in0, scalar, in1, op0, op1, reads, writes):
        S.op(eng, lambda e, out=out, in0=in0, scalar=scalar, in1=in1, op0=op0, op1=op1:
             e.scalar_tensor_tensor(out=out, in0=in0, scalar=scalar, in1=in1, op0=op0, op1=op1),
             reads=reads, writes=writes)

    def ts(eng, out, in0, s1, s2, op0, op1, reads, writes):
        if s2 is None:
            S.op(eng, lambda e, out=out, in0=in0, s1=s1, op0=op0:
                 e.tensor_scalar(out=out, in0=in0, scalar1=s1, scalar2=None, op0=op0),
                 reads=reads, writes=writes)
        else:
            S.op(eng, lambda e, out=out, in0=in0, s1=s1, s2=s2, op0=op0, op1=op1:
                 e.tensor_scalar(out=out, in0=in0, scalar1=s1, scalar2=s2, op0=op0, op1=op1),
                 reads=reads, writes=writes)

    def cp(eng, out, in_, reads, writes):
        if eng == "act":
            S.op("act", lambda e, out=out, in_=in_: e.copy(out=out, in_=in_), reads=reads, writes=writes)
        else:
            S.op(eng, lambda e, out=out, in_=in_: e.tensor_copy(out=out, in_=in_), reads=reads, writes=writes)

    def recip(out, in_, reads, writes):
        S.op("dve", lambda e, out=out, in_=in_: e.reciprocal(out=out, in_=in_), reads=reads, writes=writes)

    def dma(eng, pairs, sem, reads, writes):
        def fn(e, pairs=pairs):
            return [e.dma_start(out=o, in_=i) for o, i in pairs]
        S.op(eng, fn, reads=reads, writes=writes, dma=sem, ndma=len(pairs))

    rr = {"i": 0}

    def alt():
        rr["i"] += 1
        return "act" if rr["i"] % 2 else "dve"

    def cond_of(t):
        return 1 if t == 4 else 0

    def xk(c, t):
        return ("x", c, t)

    def hk(c, t):
        return ("h", c, t)

    dma("sp", [(spt[:], spar), (cft[:], cf_d)], "D_const", [], ["spt", "cft"])
    dma("pool", [(cbt[:], cb_d)], "D_constb", [], ["cbt"])
    act(scT[:].rearrange("p c r -> p (c r)"), spt[:, SP_CT:SP_CT + 16], AF.Silu, ["spt"], ["scT"])

    units = []

    def WK(s):
        return ("w", s)

    def make_mod_unit(l, jg):
        def load(s):
            assert l == 0
            dma("pool", [(wslot[s][:, 0:4096], wmod0[jg])], f"D_w{s}", [], [WK(s)])

        def compute(s):
            ffn_flush()
            wv = wslot[s][:, 0:4096].rearrange("p (kc n) -> p kc n", kc=8)
            bank = 7
            for cc in range(4):
                idx = jg * 4 + cc
                mm(ps[bank][:, cc * 2:cc * 2 + 2],
                   [(wv[:, kc, cc * 128:(cc + 1) * 128], scT[:, kc, :]) for kc in range(8)],
                   [WK(s), "scT"], [PK(bank)])
            for r in range(2):
                src = ps[bank][:, 0:8].rearrange("p (a r) -> p a r", r=2)[:, :, r]
                tt("dve", mT[:, l, jg * 4:jg * 4 + 4, r], src,
                   spt[:, SP_BT + l * 72 + jg * 4: SP_BT + l * 72 + jg * 4 + 4], ALU.add,
                   [PK(bank), "spt"], [("mT", l)])
        return load, compute


    bg_tasks = []

    def make_small_mod_tasks(l):
        def load(k):
            sl = k % 2
            assert l == 1
            dma("pool", [(mslot[sl], wmod1[k])], f"D_m{sl}", ["BIG"], [("mw", sl)])

        def task(k):
            sl = k % 2
            wv = mslot[sl].rearrange("p (kc n) -> p kc n", kc=8)
            mm(ps[7][:, 8:10], [(wv[:, kc, :], scT[:, kc, :]) for kc in range(8)], [("mw", sl), "scT", "BIG"], [PK(7)])
            tt("dve", mT[:, l, k, :], ps[7][:, 8:10], spt[:, SP_BT + l * 72 + k: SP_BT + l * 72 + k + 1].to_broadcast([128, 2]),
               ALU.add, [PK(7), "spt"], [("mT", l)])
            if k + 2 < 72:
                load(k + 2)
        load(0)
        load(1)
        for k in range(72):
            bg_tasks.append(lambda k=k: task(k))

    def run_bg(n):
        for _ in range(n):
            if bg_tasks:
                bg_tasks.pop(0)()

    def derive_scalars(l):
        for i in range(3):
            g = spt[:, SP_GT + (l * 3 + i) * 8: SP_GT + (l * 3 + i) * 8 + 8]
            for r in range(2):
                sh = mT[:, l, (3 * i + 0) * 8:(3 * i + 0) * 8 + 8, r]
                sc = mT[:, l, (3 * i + 1) * 8:(3 * i + 1) * 8 + 8, r]
                gt = mT[:, l, (3 * i + 2) * 8:(3 * i + 2) * 8 + 8, r]
                stt("dve", AB[:, l, 3 * i + 0, :, r], sc, 1.0, g, ALU.add, ALU.mult,
                    [("mT", l), "spt"], [("AB", l)])
                cp("dve", AB[:, l, 3 * i + 1, :, r], sh, [("mT", l)], [("AB", l)])
                ts("dve", AB[:, l, 3 * i + 2, :, r], gt, 0.5 if i != 1 else 1.0, None, ALU.mult, None,
                   [("mT", l)], [("AB", l)])

    def ABs(l, kind, c, t):
        return AB[:, l, kind, c, cond_of(t):cond_of(t) + 1]

    def norm_stage(l, i):
        for t in range(NTILE):
            tsl = slice(t * TL, (t + 1) * TL)
            bank = 6
            for c in range(8):
                sq, sqk = bfp.get()
                act(sq[:], xT[:, c, tsl], AF.Square, [xk(c, t)], [sqk])
                mm(ps[bank][:], [(ones, sq[:])], [sqk, "cbt"], [PK(bank)], start=(c == 0), stop=(c == 7))
            rs, rsk = rsp.get()
            act(rs[:], ps[bank][:], AF.Ln, [PK(bank)], [rsk], bias=EPS, scale=1.0 / D)
            act(rs[:], rs[:], AF.Exp, [rsk], [rsk], scale=-0.5)
            for c in range(8):
                tmp, tk = f32p.get()
                stt("dve", tmp[:], xT[:, c, tsl], ABs(l, 3 * i, c, t), rs[:], ALU.mult, ALU.mult,
                    [xk(c, t), rsk, ("AB", l)], [tk])
                if c % 2 == 0:
                    act(hT[:, c, tsl], tmp[:], AF.Identity, [tk, ("AB", l)], [hk(c, t)],
                        bias=ABs(l, 3 * i + 1, c, t), scale=1.0)
                else:
                    ts("pool", hT[:, c, tsl], tmp[:], ABs(l, 3 * i + 1, c, t), None, ALU.add, None,
                       [tk, ("AB", l)], [hk(c, t)])

    ffn_state = {"jc": 0, "dcc": 0, "pending": None}
    att_state = {"c": 0}
    conv_pend = {"v": None}

    def make_ffn_unit(l, i, g):
        k = l * 2 + i
        f0 = g * G
        gk = 3 * (2 * i) + 2
        n13 = 8 * 2 * G * 128

        def load(s):
            wv = wslot[s][:, 0:n13].rearrange("p (kc gu n) -> p kc gu n", kc=8, gu=2)
            pairs = []
            for gu in range(2):
                src = w13[k][:, gu * DFF + f0 * 128: gu * DFF + (f0 + G) * 128].rearrange("(kc p) n -> p kc n", p=128)
                pairs.append((wv[:, :, gu, :], src))
            w2v = wslot[s][:, n13:n13 + G * 1024].rearrange("p (j d) -> p j d", j=G)
            pairs.append((w2v, w2[k][f0 * 128:(f0 + G) * 128, :].rearrange("(j p) d -> p j d", p=128)))
            dma("pool", pairs, f"D_w{s}", [], [WK(s)])

        def compute(s):
            wv = wslot[s][:, 0:n13].rearrange("p (kc gu n) -> p kc gu n", kc=8, gu=2)
            w2v = wslot[s][:, n13:n13 + G * 1024].rearrange("p (j d) -> p j d", j=G)
            for t in range(NTILE):
                tsl = slice(t * TL, (t + 1) * TL)
                par = (g * NTILE + t) % 2
                for j in range(G):
                    jc = ffn_state["jc"]
                    ffn_state["jc"] += 1
                    bg_, bu_ = (0, 1) if jc % 2 == 0 else (2, 3)
                    mm(ps[bg_][:], [(wv[:, kc, 0, j * 128:(j + 1) * 128], hT[:, kc, tsl]) for kc in range(8)],
                       [WK(s)] + [hk(c, t) for c in range(8)], [PK(bg_)])
                    mm(ps[bu_][:], [(wv[:, kc, 1, j * 128:(j + 1) * 128], hT[:, kc, tsl]) for kc in range(8)],
                       [WK(s)] + [hk(c, t) for c in range(8)], [PK(bu_)])
                    sg, sgk = f32p.get()
                    act(sg[:], ps[bg_][:], AF.Silu, [PK(bg_)], [sgk])
                    tt("dve", actb[par][j][:], sg[:], ps[bu_][:], ALU.mult, [sgk, PK(bu_)], [("act", par, j)])
                prev = ffn_state["pending"]
                if prev is not None:
                    prev()

                def w2step(t=t, tsl=tsl, par=par, s=s, w2v=w2v):
                    for dc in range(8):
                        dcc = ffn_state["dcc"]
                        ffn_state["dcc"] += 1
                        bk = 4 + dcc % 4
                        mm(ps[bk][:], [(w2v[:, j, dc * 128:(dc + 1) * 128], actb[par][j][:]) for j in range(G)],
                           [WK(s)] + [("act", par, j) for j in range(G)], [PK(bk)])
                        stt("dve", xT[:, dc, tsl], ps[bk][:], ABs(l, gk, dc, t), xT[:, dc, tsl], ALU.mult, ALU.add,
                            [PK(bk), xk(dc, t), ("AB", l)], [xk(dc, t)])
                ffn_state["pending"] = w2step
        return load, compute

    def ffn_flush():
        prev = ffn_state["pending"]
        if prev is not None:
            prev()
        ffn_state["pending"] = None

    def wout_accum(l, wrows, on, onk, t, wkey, banks=(2, 3)):
        tsl = slice(t * TL, (t + 1) * TL)
        for dc in range(8):
            bk = banks[dc % len(banks)]
            mm(ps[bk][:], [(wrows[:, dc * 128:(dc + 1) * 128], on)], [wkey, onk], [PK(bk)])
            stt("dve", xT[:, dc, tsl], ps[bk][:], ABs(l, 5, dc, t), xT[:, dc, tsl], ALU.mult, ALU.add,
                [PK(bk), xk(dc, t), ("AB", l)], [xk(dc, t)])

    def proj_norm_tasks(l, wcols, t, gcol, wkey, do_rope, dest, destk, banks=(0, 1, 1), out=None):
        pb, sb2, rb = banks
        tsl = slice(t * TL, (t + 1) * TL)
        stt_ = {}

        def t0():
            mm(ps[pb][:], [(wcols[:, kc, :], hT[:, kc, tsl]) for kc in range(8)],
               [wkey] + [hk(c, t) for c in range(8)], [PK(pb)])
            sq, sqk = bfp.get()
            act(sq[:], ps[pb][:], AF.Square, [PK(pb)], [sqk])
            stt_["sq"] = (sq, sqk)

        def t1():
            sq, sqk = stt_["sq"]
            mm(ps[sb2][:], [(bones, sq[:])], [sqk, "cbt"], [PK(sb2)])
            rs, rsk = f32p.get()
            act(rs[:], ps[sb2][:], AF.Ln, [PK(sb2)], [rsk], bias=EPS, scale=1.0 / 64)
            act(rs[:], rs[:], AF.Exp, [rsk], [rsk], scale=-0.5)
            kn, knk = f32p.get()
            stt("dve", kn[:], ps[pb][:], spt[:, SP_QKG + gcol:SP_QKG + gcol + 1], rs[:], ALU.mult, ALU.mult,
                [PK(pb), rsk, "spt"], [knk])
            stt_["kn"] = (kn, knk)
            if out is not None:
                out["kn"] = (kn, knk)
            if not do_rope:
                cp("act", dest, kn[:], [knk, "BIG"], [destk])
            else:
                knb, knbk = bfp.get()
                cp("act", knb[:], kn[:], [knk], [knbk])
                stt_["knb"] = (knb, knbk)
                cst, cstk = f32p.get()
                snt, sntk = f32p.get()
                dma("sp", [(cst[:], cs_d[:, t * TL:(t + 1) * TL])], "D_" + str(cstk), [], [cstk])
                dma("sp", [(snt[:], cs_d[:, 2048 + t * TL: 2048 + (t + 1) * TL])], "D_" + str(sntk), [], [sntk])
                stt_["cs"] = (cst, cstk, snt, sntk)

        def t2():
            kn, knk = stt_["kn"]
            knb, knbk = stt_["knb"]
            cst, cstk, snt, sntk = stt_["cs"]
            mm(ps[rb][:], [(ropeR, knb[:])], [knbk, "cbt"], [PK(rb)])
            tt("dve", cst[:], kn[:], cst[:], ALU.mult, [knk, cstk], [cstk])
            tt("dve", snt[:], ps[rb][:], snt[:], ALU.mult, [PK(rb), sntk], [sntk])
            tt("dve", dest, cst[:], snt[:], ALU.add, [cstk, sntk, "BIG"], [destk])

        return [t0, t1] + ([t2] if do_rope else [])

    def proj_norm(l, wcols, t, gcol, wkey, do_rope, dest=None, destk=None, banks=(0, 1, 1)):
        if dest is None:
            ob, obk = bfp.get()
            dest = ob[:]
            destk = obk
        out = {}
        for f in proj_norm_tasks(l, wcols, t, gcol, wkey, do_rope, dest, destk, banks=banks, out=out):
            f()
        kn, knk = out["kn"]
        return kn, knk, dest, destk


    def k_stage_pipelined(l, kcols, gcol, wkey, ctx_dst, col0, ncol):
        tl = []
        outs = []
        for t in range(NTILE):
            kd = kT[:, 512 + t * TL: 512 + (t + 1) * TL] if t < 4 else kT[:, 2560:3072]
            o = {}
            bk = (0, 1, 1) if t % 2 == 0 else (2, 3, 3)
            tl.append(proj_norm_tasks(l, kcols, t, gcol, wkey, t < 4, kd, ("kT", t), banks=bk, out=o))
            outs.append(o)
        for k in range(NTILE + 2):
            for d in range(3):
                t = k - d
                if 0 <= t < NTILE and d < len(tl[t]):
                    tl[t][d]()
                    if t == 4 and d == 1:
                        kn, knk = outs[4]["kn"]
                        store_ctx_k(kn, knk, ctx_dst, col0, ncol)

    def store_ctx_k(kn, knk, dst, col0, ncol):
        for blk in range(4):
            tr(ps[5][:, blk * 128:(blk + 1) * 128], kn[:, blk * 128:(blk + 1) * 128], identf, [knk, "cft"], [PK(5)])
        stg, stgk = f32p.get()
        cp("act", stg[:], ps[5][:], [PK(5)], [stgk])
        src = stg[:].rearrange("p (b f) -> p b f", b=4)[:, :, 0:ncol]
        d = dst[:, col0:col0 + ncol].rearrange("(b p) f -> p b f", p=128)
        dma("sp", [(d, src)], "D_" + str(stgk), [stgk], [])

    def load_cache_kT(src_ap, ndup, sem):
        stg, stgk = f32p.get()
        sv = stg[:].rearrange("p (b f) -> p b f", b=4)
        pairs = []
        w = 128 // ndup
        for dd in range(ndup):
            pairs.append((sv[:, :, dd * w:(dd + 1) * w], src_ap.rearrange("(b p) f -> p b f", p=128)))
        dma("sp", pairs, "D_" + str(stgk), [], [stgk])
        for blk in range(4):
            tr(ps[4][:, blk * 128:(blk + 1) * 128], stg[:, blk * 128:(blk + 1) * 128], identf, [stgk, "cft"], [PK(4)])
        cp("dve", kT[:, 0:512], ps[4][:], [PK(4), "BIG"], ["kT_cache"])

    def v_stage(wv_cols, wkey, ctx_dst, col0, ncol, odd=False):
        for bg4 in range(5):
            bank = 2 + bg4 % 4
            for b4 in range(4):
                blk = bg4 * 4 + b4
                mm(ps[bank][:, b4 * 128:(b4 + 1) * 128],
                   [(hT[:, kc, blk * 128:(blk + 1) * 128], wv_cols[:, kc, :]) for kc in range(8)],
                   [wkey] + [hk(c, bg4) for c in range(8)], [PK(bank)])
            pv = ps[bank][:].rearrange("p (b f) -> p b f", b=4)
            b0 = 4 + bg4 * 4
            if not odd:
                cp(alt(), v3[:, b0:b0 + 4, 0:128], pv, [PK(bank), "BIG"], [("v", bg4)])
            else:
                cp("act", v3[:, b0:b0 + 4, 0:64], pv[:, :, 0:64], [PK(bank), "BIG"], [("v", bg4)])
                cp("dve", v3[:, b0:b0 + 4, 128:192], pv[:, :, 64:128], [PK(bank), "BIG", ("v", bg4)], [("v", bg4)])
            if bg4 == 4:
                stg, stgk = f32p.get()
                cp("act", stg[:], ps[bank][:], [PK(bank)], [stgk])
                src = stg[:].rearrange("p (b f) -> p b f", b=4)[:, :, 0:ncol]
                d = ctx_dst[:, col0:col0 + ncol].rearrange("(b p) f -> p b f", p=128)
                dma("sp", [(d, src)], "D_" + str(stgk), [stgk], [])

    L0 = 0

    def make_even_attn_unit(h):
        def load(s):
            wv = wslot[s][:, 0:3072].rearrange("p (kc a n) -> p kc a n", kc=8, a=3)
            pairs = []
            for a in range(3):
                src = ewin[:, a * 512 + h * 128: a * 512 + (h + 1) * 128].rearrange("(kc p) n -> p kc n", p=128)
                pairs.append((wv[:, :, a, :], src))
            pairs.append((wslot[s][:, 3072:4096], ewout[h * 128:(h + 1) * 128, :]))
            dma("pool", pairs, f"D_w{s}", [], [WK(s)])

        def compute(s):
            l = L0
            wv = wslot[s][:, 0:3072].rearrange("p (kc a n) -> p kc a n", kc=8, a=3)
            wrows = wslot[s][:, 3072:4096]
            load_cache_kT(cek[:, h * 128:(h + 1) * 128], 1, None)
            dma("pool", [(v3[:, 0:4, 0:128], cev[:, h * 128:(h + 1) * 128].rearrange("(b p) f -> p b f", p=128))],
                "D_vc", ["BIG"], [("v", "c")])
            k_stage_pipelined(l, wv[:, :, 1, :], 1, WK(s), nek, h * 128, 128)
            import os
            _dbg = int(os.environ.get("KDBG", "99"))
            if h >= 1 and _dbg <= 1:
                return
            v_stage(wv[:, :, 2, :], WK(s), nev, h * 128, 128)
            if h >= 1 and _dbg <= 2:
                return
            def qchain(t_):
                qt_, qtk_ = qtp.get()
                return (proj_norm_tasks(l, wv[:, :, 0, :], t_, 0, WK(s), t_ < 4, qt_[:], qtk_, banks=(0, 1, 1)), qt_[:], qtk_)

            tasks, qT, qTk = qchain(0)
            for f in tasks:
                f()
            for t in range(NTILE):
                if t < 4:
                    segs = [(0, TL, [(kc * 128, kc, ["kT_cache", ("v", "c")] if kc < 4 else
                                      [("kT", (kc - 4) // 4), ("v", (kc - 4) // 4)]) for kc in range(20)])]
                else:
                    segs = [(sq_ * 256, 256, [(2560 + sq_ * 256 + k2 * 128, 20 + sq_ * 2 + k2, [("kT", 4), ("v", 4)])
                                              for k2 in range(2)]) for sq_ in range(2)]
                for q0, qn, chunks in segs:
                    nck = len(chunks)
                    pend = None
                    for ci in range(nck + 1):
                        cur = None
                        if ci < nck:
                            kc0, vb, rkeys = chunks[ci]
                            sb = (att_state["c"] % 2) * 2
                            att_state["c"] += 1
                            mm(ps[sb][:, 0:qn], [(kT[0:64, kc0:kc0 + 128], qT[0:64, q0:q0 + qn])], [qTk] + rkeys, [PK(sb)])
                            mm(ps[sb + 1][:, 0:qn], [(kT[64:128, kc0:kc0 + 128], qT[64:128, q0:q0 + qn])], [qTk] + rkeys, [PK(sb + 1)])
                            p1, p1k = bfp.get()
                            act(p1[:, 0:qn], ps[sb][:, 0:qn], AF.Exp, [PK(sb)], [p1k], scale=0.125)
                            p2, p2k = bfp.get()
                            act(p2[:, 0:qn], ps[sb + 1][:, 0:qn], AF.Exp, [PK(sb + 1)], [p2k], scale=0.125)
                            cur = (p1, p1k, p2, p2k, vb, rkeys, ci)
                        if pend is not None:
                            p1, p1k, p2, p2k, vb, rkeys, cj = pend
                            st_, sp_ = (cj == 0), (cj == nck - 1)
                            mm(ps[4][:, q0:q0 + qn], [(vsb[:, vb, :], p1[:, 0:qn])], [p1k] + rkeys, [PK(4)], st_, sp_)
                            mm(ps[6][:, q0:q0 + qn], [(ones, p1[:, 0:qn])], [p1k, "cbt"], [PK(6)], st_, sp_)
                            mm(ps[5][:, q0:q0 + qn], [(vsb[:, vb, :], p2[:, 0:qn])], [p2k] + rkeys, [PK(5)], st_, sp_)
                            mm(ps[7][:, q0:q0 + qn], [(ones, p2[:, 0:qn])], [p2k, "cbt"], [PK(7)], st_, sp_)
                        pend = cur
                ntasks = []
                if t + 1 < NTILE:
                    ntasks, nqT, nqTk = qchain(t + 1)
                if ntasks:
                    ntasks.pop(0)()
                r1, r1k = f32p.get()
                recip(r1[:], ps[6][:], [PK(6)], [r1k])
                a1, a1k = f32p.get()
                tt("dve", a1[:], ps[4][:], r1[:], ALU.mult, [PK(4), r1k], [a1k])
                r2, r2k = f32p.get()
                recip(r2[:], ps[7][:], [PK(7)], [r2k])
                a2, a2k = f32p.get()
                tt("dve", a2[:], ps[5][:], r2[:], ALU.mult, [PK(5), r2k], [a2k])
                stt("dve", a1[:], a2[:], misc[:, 0:1], a1[:], ALU.mult, ALU.add, [a2k, a1k, "misc"], [a1k])
                if ntasks:
                    ntasks.pop(0)()
                sq, sqk = bfp.get()
                act(sq[:], a1[:], AF.Square, [a1k], [sqk])
                mm(ps[2][:], [(ones, sq[:])], [sqk, "cbt"], [PK(2)])
                rs, rsk = f32p.get()
                act(rs[:], ps[2][:], AF.Ln, [PK(2)], [rsk], bias=EPS, scale=1.0 / 128)
                act(rs[:], rs[:], AF.Exp, [rsk], [rsk], scale=-0.5)
                on, onk = bfp.get()
                stt("dve", on[:], a1[:], misc[:, 1:2], rs[:], ALU.mult, ALU.mult, [a1k, rsk, "misc"], [onk])
                while ntasks:
                    ntasks.pop(0)()
                wout_accum(l, wrows, on[:], onk, t, WK(s), banks=(4, 5, 6, 7))
                if t + 1 < NTILE:
                    qT, qTk = nqT, nqTk
        return load, compute

    def useg(t):
        if t < 4:
            return [(0, TL, t * TL + 1)]
        return [(0, 256, 2048 + 2), (256, 256, 2304 + 3)]

    def make_even_conv_unit(j):
        def load(s):
            wv = wslot[s][:, 0:3072].rearrange("p (kc a n) -> p kc a n", kc=8, a=3)
            pairs = []
            for a in range(3):
                src = ewin[:, 1536 + a * 512 + j * 128: 1536 + a * 512 + (j + 1) * 128].rearrange("(kc p) n -> p kc n", p=128)
                pairs.append((wv[:, :, a, :], src))
            pairs.append((wslot[s][:, 3072:4096], ewout[512 + j * 128: 512 + (j + 1) * 128, :]))
            dma("pool", pairs, f"D_w{s}", [], [WK(s)])

        def compute(s):
            l = L0
            wv = wslot[s][:, 0:3072].rearrange("p (kc a n) -> p kc a n", kc=8, a=3)
            wrows = wslot[s][:, 3072:4096]
            for t in range(NTILE):
                tsl = slice(t * TL, (t + 1) * TL)
                rd = [WK(s)] + [hk(c, t) for c in range(8)]
                b0_ = (t % 2) * 2
                mm(ps[b0_][:], [(wv[:, kc, 1, :], hT[:, kc, tsl]) for kc in range(8)], rd, [PK(b0_)])
                mm(ps[b0_ + 1][:], [(wv[:, kc, 2, :], hT[:, kc, tsl]) for kc in range(8)], rd, [PK(b0_ + 1)])
                xs, xsk = f32p.get()
                cp("act", xs[:], ps[b0_ + 1][:], [PK(b0_ + 1)], [xsk])
                for (o0, n, uc) in useg(t):
                    tt("dve", ubuf[:, uc:uc + n], ps[b0_][:, o0:o0 + n], xs[:, o0:o0 + n], ALU.mult,
                       [PK(b0_), xsk, "upad", "BIG"], [("u", t)])
                run_bg(2)
            for t in range(NTILE):
                tsl = slice(t * TL, (t + 1) * TL)
                rd = [WK(s)] + [hk(c, t) for c in range(8)]
                bk = t % 3
                mm(ps[bk][:], [(wv[:, kc, 0, :], hT[:, kc, tsl]) for kc in range(8)], rd, [PK(bk)])
                acc, acck = f32p.get()
                uk = [("u", tt_) for tt_ in range(NTILE)] + ["upad", "BIG"]
                for (o0, n, uc) in useg(t):
                    ts("dve", acc[:, o0:o0 + n], ubuf[:, uc - 1:uc - 1 + n], spt[:, SP_CW + j * 3: SP_CW + j * 3 + 1], None,
                       ALU.mult, None, uk + ["spt"], [acck])
                    stt("dve", acc[:, o0:o0 + n], ubuf[:, uc:uc + n], spt[:, SP_CW + j * 3 + 1: SP_CW + j * 3 + 2],
                        acc[:, o0:o0 + n], ALU.mult, ALU.add, uk + [acck, "spt"], [acck])
                    stt("dve", acc[:, o0:o0 + n], ubuf[:, uc + 1:uc + 1 + n], spt[:, SP_CW + j * 3 + 2: SP_CW + j * 3 + 3],
                        acc[:, o0:o0 + n], ALU.mult, ALU.add, uk + [acck, "spt"], [acck])
                y, yk = bfp.get()
                tt("dve", y[:], ps[bk][:], acc[:], ALU.mult, [PK(bk), acck], [yk])
                if conv_pend["v"] is not None:
                    y_, yk_, t_ = conv_pend["v"]
                    wout_accum(l, wrows, y_[:], yk_, t_, WK(s), banks=(3, 4, 5, 6))
                conv_pend["v"] = (y, yk, t)
                run_bg(2)
            y_, yk_, t_ = conv_pend["v"]
            wout_accum(l, wrows, y_[:], yk_, t_, WK(s), banks=(3, 4, 5, 6))
            conv_pend["v"] = None
        return load, compute

    L1 = 1

    def make_odd_unit(g):
        def load(s):
            qv = wslot[s][:, 0:2048].rearrange("p (kc n) -> p kc n", kc=8)
            kv = wslot[s][:, 2048:3072].rearrange("p (kc n) -> p kc n", kc=8)
            vv = wslot[s][:, 3072:4096].rearrange("p (kc n) -> p kc n", kc=8)
            ov = wslot[s][:, 4096:6144].rearrange("p (j d) -> p j d", j=2)
            pairs = [(qv, owin[:, g * 256:(g + 1) * 256].rearrange("(kc p) n -> p kc n", p=128))]
            for dd in range(2):
                pairs.append((kv[:, :, dd * 64:(dd + 1) * 64],
                              owin[:, 1024 + g * 64: 1024 + (g + 1) * 64].rearrange("(kc p) n -> p kc n", p=128)))
                pairs.append((vv[:, :, dd * 64:(dd + 1) * 64],
                              owin[:, 1280 + g * 64: 1280 + (g + 1) * 64].rearrange("(kc p) n -> p kc n", p=128)))
            pairs.append((ov, owout[g * 256:(g + 1) * 256, :].rearrange("(j p) d -> p j d", p=128)))
            dma("pool", pairs, f"D_w{s}", [], [WK(s)])

        def compute(s):
            l = L1
            qv = wslot[s][:, 0:2048].rearrange("p (kc n) -> p kc n", kc=8)
            kv = wslot[s][:, 2048:3072].rearrange("p (kc n) -> p kc n", kc=8)
            vv = wslot[s][:, 3072:4096].rearrange("p (kc n) -> p kc n", kc=8)
            ov = wslot[s][:, 4096:6144].rearrange("p (j d) -> p j d", j=2)
            BK = (6, 7, 6)
            if g == 0:
                S.op("pool", lambda e: e.memset(v3[:, :, 64:128], 1.0), reads=["BIG"],
                     writes=[("v", i) for i in range(5)] + [("v", "c")])
            load_cache_kT(cok[:, g * 64:(g + 1) * 64], 2, None)
            vsrc = cov[:, g * 64:(g + 1) * 64].rearrange("(b p) f -> p b f", p=128)
            dma("pool", [(v3[:, 0:4, 0:64], vsrc), (v3[:, 0:4, 128:192], vsrc)], "D_vc", ["BIG"], [("v", "c")])
            k_stage_pipelined(l, kv, 3, WK(s), nok, g * 64, 64)
            v_stage(vv, WK(s), nov, g * 64, 64, odd=True)

            hooks = []

            def run_hooks(n):
                for _ in range(n):
                    if hooks:
                        hooks.pop(0)()

            def qchain(i):
                pr_, t_ = divmod(i, NTILE)
                qt_, qtk_ = qtp.get()
                tasks = proj_norm_tasks(l, qv[:, :, pr_ * 128:(pr_ + 1) * 128], t_, 2, WK(s), t_ < 4,
                                        qt_[:], qtk_, banks=BK)
                return tasks, qt_[:], qtk_

            wtog = {"v": 0}

            def wout_task(pr_, t_, dc, on, onk):
                tsl = slice(t_ * TL, (t_ + 1) * TL)
                if any(getattr(h_, "is_chain", False) for h_ in hooks):
                    bk = 7
                else:
                    wtog["v"] ^= 1
                    bk = 6 + wtog["v"]
                mm(ps[bk][:], [(ov[:, pr_, dc * 128:(dc + 1) * 128], on)], [WK(s), onk], [PK(bk)])
                stt("dve", xT[:, dc, tsl], ps[bk][:], ABs(l, 5, dc, t_), xT[:, dc, tsl], ALU.mult, ALU.add,
                    [PK(bk), xk(dc, t_), ("AB", l)], [xk(dc, t_)])

            tasks, qT, qTk = qchain(0)
            for f in tasks:
                f()
            NI = 2 * NTILE
            for i in range(NI):
                pr, t = divmod(i, NTILE)
                pairidx = g * 2 + pr
                if i + 1 < NI:
                    ntasks, nqT, nqTk = qchain(i + 1)
                    for f_ in ntasks:
                        f_.is_chain = True
                    old = list(hooks)
                    del hooks[:]
                    while ntasks or old:
                        if ntasks:
                            hooks.append(ntasks.pop(0))
                        if old:
                            hooks.append(old.pop(0))
                if t < 4:
                    chunks = [(kc * 128, kc, ["kT_cache", ("v", "c")], None) for kc in range(4)]
                    for jb in range(max(0, 4 * t - 1), min(15, 4 * t + 4) + 1):
                        chunks.append((512 + jb * 128, 4 + jb, [("kT", jb // 4), ("v", jb // 4)], jb - 4 * t + 1))
                    segs = [(0, TL, chunks)]
                else:
                    segs = [(sq_ * 256, 256, [(2560 + sq_ * 256 + k2 * 128, 20 + sq_ * 2 + k2, [("kT", 4), ("v", 4)], None)
                                              for k2 in range(2)]) for sq_ in range(2)]
                for q0, qn, chunks in segs:
                    nck = len(chunks)
                    pend = None
                    for ci in range(nck + 1):
                        cur = None
                        if ci < nck:
                            kc0, vb, rkeys, mi = chunks[ci]
                            sb = (att_state["c"] % 2) * 2
                            att_state["c"] += 1
                            if mi is None:
                                c0, c1 = q0, q0 + qn
                            else:
                                jrel = mi - 1
                                c0, c1 = max(0, jrel - 1) * 128, (min(3, jrel + 1) + 1) * 128
                            cw = c1 - c0
                            mm(ps[sb][:, 0:cw], [(kT[0:64, kc0:kc0 + 128], qT[0:64, c0:c1])], [qTk] + rkeys, [PK(sb)])
                            mm(ps[sb + 1][:, 0:cw], [(kT[64:128, kc0:kc0 + 128], qT[64:128, c0:c1])], [qTk] + rkeys, [PK(sb + 1)])
                            p1, p1k = bfp.get()
                            act(p1[:, 0:cw], ps[sb][:, 0:cw], AF.Exp, [PK(sb)], [p1k], scale=0.125)
                            p2, p2k = bfp.get()
                            act(p2[:, 0:cw], ps[sb + 1][:, 0:cw], AF.Exp, [PK(sb + 1)], [p2k], scale=0.125)
                            if mi is not None:
                                for qb in range(c0 // 128, c1 // 128):
                                    if qb == jrel + 1:
                                        tri = cbt[:, CB_MASK: CB_MASK + 128]
                                    elif qb == jrel - 1:
                                        tri = cbt[:, CB_MASK + 5 * 512 + 384: CB_MASK + 6 * 512]
                                    else:
                                        continue
                                    o_ = qb * 128 - c0
                                    tt("dve", p1[:, o_:o_ + 128], p1[:, o_:o_ + 128], tri, ALU.mult, [p1k, "cbt"], [p1k])
                                    tt("dve", p2[:, o_:o_ + 128], p2[:, o_:o_ + 128], tri, ALU.mult, [p2k, "cbt"], [p2k])
                            cur = (p1, p1k, p2, p2k, vb, rkeys, ci, c0, c1)
                        if pend is not None:
                            p1, p1k, p2, p2k, vb, rkeys, cj, c0, c1 = pend
                            cw = c1 - c0
                            st_, sp_ = (cj == 0), (cj == nck - 1)
                            mm(ps[4][:, c0:c1], [(v3[:, vb, 0:128], p1[:, 0:cw])], [p1k] + rkeys, [PK(4)], st_, sp_)
                            mm(ps[5][:, c0:c1], [(v3[:, vb, 64:192], p2[:, 0:cw])], [p2k] + rkeys, [PK(5)], st_, sp_)
                        pend = cur
                        run_hooks(2)
                run_hooks(1000)
                zc, zck = f32p.get()
                cp("dve", zc[0:64, :], ps[4][64:128, :], [PK(4)], [zck])
                cp("dve", zc[64:128, :], ps[5][0:64, :], [PK(5), zck], [zck])
                ts("dve", zc[:], zc[:], misc[:, 24 + pairidx:25 + pairidx], None, ALU.add, None, [zck, "misc"], [zck])
                recip(zc[:], zc[:], [zck], [zck])
                on, onk = onp.get()
                tt("dve", on[0:64, :], ps[4][0:64, :], zc[0:64, :], ALU.mult, [PK(4), zck], [onk])
                tt("dve", on[64:128, :], ps[5][64:128, :], zc[64:128, :], ALU.mult, [PK(5), zck, onk], [onk])
                for dc in range(8):
                    hooks.append(lambda pr=pr, t=t, dc=dc, on=on, onk=onk: wout_task(pr, t, dc, on[:], onk))
                if i + 1 < NI:
                    qT, qTk = nqT, nqTk
            run_hooks(1000)
        return load, compute

    def plain(fn):
        return (None, lambda s, fn=fn: fn())

    def load_x():
        for b in range(NBLK):
            t = b // 4
            for half in range(2):
                stg, stgk = f32p.get()
                dma("sp", [(stg[:], xin[b * 128:(b + 1) * 128, half * 512:(half + 1) * 512])], "D_" + str(stgk), [], [stgk])
                bank = (b * 2 + half) % 4
                for j in range(4):
                    tr(ps[bank][:, j * 128:(j + 1) * 128], stg[:, j * 128:(j + 1) * 128], identf, [stgk, "cft"], [PK(bank)])
                cp(alt(), xT[:, half * 4:half * 4 + 4, b * 128:(b + 1) * 128],
                   ps[bank][:].rearrange("p (j n) -> p j n", j=4), [PK(bank)],
                   [xk(c, t) for c in range(half * 4, half * 4 + 4)])

    def store_y():
        for b in range(NBLK):
            t = b // 4
            for half in range(2):
                bank = (b * 2 + half) % 4
                for j in range(4):
                    c = half * 4 + j
                    tr(ps[bank][:, j * 128:(j + 1) * 128], xT[:, c, b * 128:(b + 1) * 128], identf,
                       [xk(c, t), "cft"], [PK(bank)])
                stg, stgk = f32p.get()
                cp(alt(), stg[:], ps[bank][:], [PK(bank)], [stgk])
                if b < 16:
                    d = ys[b * 128:(b + 1) * 128, half * 512:(half + 1) * 512]
                else:
                    d = yp[(b - 16) * 128:(b - 15) * 128, half * 512:(half + 1) * 512]
                dma("sp", [(d, stg[:])], "D_" + str(stgk), [stgk], [])

    def setup_misc():
        lam_init = 0.8 - 0.6 * float(np.exp(-0.3 * 0))
        lm = spt[:, SP_LAM:SP_LAM + 256]
        pr_, prk = f32p.get()
        tt("dve", pr_[:, 0:64], lm[:, 0:64], lm[:, 64:128], ALU.mult, ["spt"], [prk])
        tt("dve", pr_[:, 64:128], lm[:, 128:192], lm[:, 192:256], ALU.mult, ["spt", prk], [prk])
        S.op("dve", lambda e: e.reduce_sum(out=misc[:, 2:3], in_=pr_[:, 0:64], axis=AX.X), reads=[prk], writes=["m2"])
        S.op("dve", lambda e: e.reduce_sum(out=misc[:, 3:4], in_=pr_[:, 64:128], axis=AX.X), reads=[prk], writes=["m3"])
        act(misc[:, 4:6], misc[:, 2:4], AF.Exp, ["m2", "m3"], ["m45"])
        tt("dve", misc[:, 6:7], misc[:, 5:6], misc[:, 4:5], ALU.subtract, ["m45"], ["m6"])
        ts("dve", misc[:, 0:1], misc[:, 6:7], -lam_init, None, ALU.add, None, ["m6"], ["misc0"])
        ts("dve", misc[:, 1:2], spt[:, SP_SUB:SP_SUB + 1], 1.0 - lam_init, None, ALU.mult, None, ["spt"], ["misc1"])
        act(misc[:, 8:32], spt[:, SP_SINK:SP_SINK + 24], AF.Exp, ["spt"], ["misc8"])
        S.op("pool", lambda e: e.memset(ubuf[:, 0:1], 0.0), writes=["up0"])
        S.op("pool", lambda e: e.memset(ubuf[:, 2049:2050], 0.0), writes=["up1"])
        S.op("pool", lambda e: e.memset(ubuf[:, 2306:2307], 0.0), writes=["up2"])
        S.op("pool", lambda e: e.memset(ubuf[:, 2563:2564], 0.0), writes=["up3"])
        S.op("dve", lambda e: e.memset(misc[:, 34:35], 0.0),
             reads=["misc0", "misc1", "misc8", "up0", "up1", "up2", "up3"], writes=["misc", "upad", "BIG"])

    seq = []
    seq.append(plain(load_x))
    for jg in range(18):
        seq.append(make_mod_unit(0, jg))
    seq.append(plain(lambda: derive_scalars(0)))
    seq.append(plain(setup_misc))
    seq.append(plain(lambda: norm_stage(0, 0)))
    for g in range(NFG):
        seq.append(make_ffn_unit(0, 0, g))
    seq.append(plain(ffn_flush))
    seq.append(plain(lambda: norm_stage(0, 1)))
    seq.append(plain(lambda: make_small_mod_tasks(1)))
    for j in range(4):
        seq.append(make_even_conv_unit(j))
    seq.append(plain(lambda: run_bg(1000)))
    seq.append(plain(lambda: S.op("dve", lambda e: e.memset(misc[:, 35:36], 0.0), reads=[], writes=["BIG"])))
    for h in range(4):
        seq.append(make_even_attn_unit(h))
    seq.append(plain(lambda: norm_stage(0, 2)))
    for g in range(NFG):
        seq.append(make_ffn_unit(0, 1, g))
    seq.append(plain(ffn_flush))
    seq.append(plain(lambda: derive_scalars(1)))
    seq.append(plain(lambda: norm_stage(1, 0)))
    for g in range(NFG):
        seq.append(make_ffn_unit(1, 0, g))
    seq.append(plain(ffn_flush))
    seq.append(plain(lambda: norm_stage(1, 1)))
    for g in range(4):
        seq.append(make_odd_unit(g))
    seq.append(plain(lambda: norm_stage(1, 2)))
    for g in range(NFG):
        seq.append(make_ffn_unit(1, 1, g))
    seq.append(plain(ffn_flush))
    seq.append(plain(store_y))

    widx = [i for i, u in enumerate(seq) if u[0] is not None]
    slot_of = {i: n % 2 for n, i in enumerate(widx)}
    nxt = {widx[n]: widx[n + 1] for n in range(len(widx) - 1)}
    if widx:
        seq[widx[0]][0](slot_of[widx[0]])
    import os
    _lim = int(os.environ.get("KSEQ_LIMIT", "100000"))
    for i, (ld, comp) in enumerate(seq):
        if i >= _lim:
            break
        if ld is not None:
            comp(slot_of[i])
            if i in nxt:
                seq[nxt[i]][0](slot_of[nxt[i]])
        else:
            comp(None)
    for _ in range(int(os.environ.get("KPAD", "0"))):
        S.op("dve", lambda e: e.memset(misc[:, 40:41], 0.0), reads=[], writes=["pad"])
        S.op("act", lambda e: e.copy(out=misc[:, 42:43], in_=misc[:, 40:41]), reads=["pad"], writes=["pad2"])
    if os.environ.get("KSEQ_PRINT"):
        print("SEM COUNTS", S.cnt)
        print("OPS", {e: len(v) for e, v in S.streams.items()}, "WAITS", {e: sum(len(w[0]) for w in v) for e, v in S.streams.items()})

    S.emit(nc, st)
    st.close()
    return nc


def _consts():
    cb = np.zeros((128, NCB), np.float32)
    p = np.arange(128)
    cb[:, CB_BONES:CB_BONES + 128] = (p[:, None] // 64 == p[None, :] // 64)
    cb[:, CB_ONES:CB_ONES + 128] = 1.0
    R = np.zeros((128, 128), np.float32)
    for blk in range(2):
        o = blk * 64
        for i in range(16):
            R[o + 16 + i, o + i] = -1.0
            R[o + i, o + 16 + i] = 1.0
            R[o + 48 + i, o + 32 + i] = -1.0
            R[o + 32 + i, o + 48 + i] = 1.0
    cb[:, CB_ROPE:CB_ROPE + 128] = R
    cb[:, CB_IDENT:CB_IDENT + 128] = np.eye(128)
    for mi in range(6):
        m = np.zeros((128, 512), np.float32)
        jrel = mi - 1
        for qb in range(4):
            ql = np.arange(128)[None, :]
            kl = np.arange(128)[:, None]
            if qb == jrel:
                blk = np.ones((128, 128))
            elif qb == jrel + 1:
                blk = (kl >= ql)
            elif qb == jrel - 1:
                blk = (kl <= ql)
            else:
                blk = np.zeros((128, 128))
            m[:, qb * 128:(qb + 1) * 128] = blk
        cb[:, CB_MASK + mi * 512: CB_MASK + (mi + 1) * 512] = m
    cf = np.zeros((128, NCF), np.float32)
    cf[:, CF_IDENT:CF_IDENT + 128] = np.eye(128)
    n = 2048
    rows = n // 64
    row = np.repeat(np.arange(rows, dtype=np.float32), 64)
    col = np.tile(np.arange(64, dtype=np.float32), rows)
    inv = (np.float32(10000.0) ** (-np.arange(0, 32, 2, dtype=np.float32) / np.float32(32))).astype(np.float32)
    ang_r = row[:, None] * inv[None, :]
    ang_c = col[:, None] * inv[None, :]
    ang = np.concatenate([ang_r, ang_r, ang_c, ang_c], axis=-1).astype(np.float32)
    cos = np.cos(ang).astype(np.float32).T
    sin = np.sin(ang).astype(np.float32).T
    cs = np.zeros((128, 4096), np.float32)
    cs[:, 0:2048] = np.concatenate([cos, cos], axis=0)
    cs[:, 2048:4096] = np.concatenate([sin, sin], axis=0)
    return cb, cf, cs


_NC_CACHE = {}


def kernel(x_prompt, x_sample, cache_even_k, cache_even_v, cache_odd_k, cache_odd_v, c, c_ctx,
           w_mod, b_mod, norm_g, ffn_w13, ffn_w2,
           even_w_in, even_w_out, even_qk_norm, even_lambda, even_subln, even_conv_w,
           odd_w_in, odd_w_out, odd_qk_norm, odd_sink, _cores=8):
    f = lambda a: np.ascontiguousarray(np.asarray(a, dtype=np.float32))
    x_prompt, x_sample = f(x_prompt), f(x_sample)
    cache_even_k, cache_even_v, cache_odd_k, cache_odd_v = map(f, (cache_even_k, cache_even_v, cache_odd_k, cache_odd_v))
    c, c_ctx, w_mod, b_mod, norm_g, ffn_w13, ffn_w2 = map(f, (c, c_ctx, w_mod, b_mod, norm_g, ffn_w13, ffn_w2))
    even_w_in, even_w_out, even_qk_norm, even_lambda, even_subln, even_conv_w = map(
        f, (even_w_in, even_w_out, even_qk_norm, even_lambda, even_subln, even_conv_w))
    odd_w_in, odd_w_out, odd_qk_norm, odd_sink = map(f, (odd_w_in, odd_w_out, odd_qk_norm, odd_sink))

    cb, cf, cs = _consts()
    if "nc" not in _NC_CACHE:
        _NC_CACHE["nc"] = build_nc()
    nc = _NC_CACHE["nc"]

    shared = {
        "cb": cb, "cf": cf, "cs": cs,
        "wmod0": np.ascontiguousarray(w_mod[0].reshape(8, 128, 18, 512).transpose(2, 1, 0, 3)).reshape(18, 128, 4096),
        "wmod1": np.ascontiguousarray(w_mod[1].reshape(8, 128, 72, 128).transpose(2, 1, 0, 3)).reshape(72, 128, 1024),
        "ewin": even_w_in[0], "ewout": even_w_out[0], "owin": odd_w_in[0], "owout": odd_w_out[0],
    }
    for l in range(2):
        for i in range(2):
            shared[f"w13_{l * 2 + i}"] = ffn_w13[l, i]
            shared[f"w2_{l * 2 + i}"] = ffn_w2[l, i]

    p = np.arange(128)
    in_maps = []
    for core in range(_cores):
        sp = np.zeros((128, NSP), np.float32)
        cvec = np.stack([c[core], c_ctx], axis=0)
        sp[:, SP_CT:SP_CT + 16] = cvec.reshape(2, 8, 128).transpose(2, 1, 0).reshape(128, 16)
        sp[:, SP_BT:SP_BT + 144] = b_mod.reshape(2, 72, 128).transpose(2, 0, 1).reshape(128, 144)
        sp[:, SP_GT:SP_GT + 48] = norm_g.reshape(6, 8, 128).transpose(2, 0, 1).reshape(128, 48)
        sp[:, SP_CW:SP_CW + 12] = even_conv_w[0].reshape(3, 4, 128).transpose(2, 1, 0).reshape(128, 12)
        sp[:, SP_QKG + 0] = even_qk_norm[0, 0][p % 64]
        sp[:, SP_QKG + 1] = even_qk_norm[0, 1][p % 64]
        sp[:, SP_QKG + 2] = odd_qk_norm[0, 0][p % 64]
        sp[:, SP_QKG + 3] = odd_qk_norm[0, 1][p % 64]
        sp[:, SP_SUB] = even_subln[0]
        sp[:, SP_LAM:SP_LAM + 256] = even_lambda[0].reshape(1, 256)
        sp[:, SP_SINK:SP_SINK + 16] = odd_sink[0].reshape(1, 16)
        sp[0:64, SP_SINKP:SP_SINKP + 8] = odd_sink[0][0::2].reshape(1, 8)
        sp[64:128, SP_SINKP:SP_SINKP + 8] = odd_sink[0][1::2].reshape(1, 8)
        m = dict(shared)
        m["xin"] = np.concatenate([x_sample[core], x_prompt[2 * core], x_prompt[2 * core + 1]], axis=0)
        m["cek"] = cache_even_k[core, 0].reshape(512, 512)
        m["cev"] = cache_even_v[core, 0].reshape(512, 512)
        m["cok"] = cache_odd_k[core, 0].reshape(512, 256)
        m["cov"] = cache_odd_v[core, 0].reshape(512, 256)
        m["spar"] = sp
        in_maps.append(m)

    res = run_bass_kernel_spmd(nc, in_maps, core_ids=list(range(_cores)))
    R = res.results
    nb = 2 * _cores
    y_prompt = np.zeros((nb, 256, D), np.float32)
    y_sample = np.zeros((_cores, 2048, D), np.float32)
    nek = np.zeros((nb, 1, 256, 4, 128), np.float32)
    nev = np.zeros((nb, 1, 256, 4, 128), np.float32)
    nok = np.zeros((nb, 1, 256, 4, 64), np.float32)
    nov = np.zeros((nb, 1, 256, 4, 64), np.float32)
    for core in range(_cores):
        r = R[core]
        y_sample[core] = r["ys"]
        y_prompt[2 * core:2 * core + 2] = r["yp"].reshape(2, 256, D)
        nek[2 * core:2 * core + 2, 0] = r["nek"].reshape(2, 256, 4, 128)
        nev[2 * core:2 * core + 2, 0] = r["nev"].reshape(2, 256, 4, 128)
        nok[2 * core:2 * core + 2, 0] = r["nok"].reshape(2, 256, 4, 64)
        nov[2 * core:2 * core + 2, 0] = r["nov"].reshape(2, 256, 4, 64)
    return (y_prompt, y_sample, nek, nev, nok, nov)
```

```python
import numpy as np
from contextlib import ExitStack
import concourse.bass as bass
import concourse.mybir as mybir
from concourse.bass_utils import run_bass_kernel_spmd

F32 = mybir.dt.float32
BF16 = mybir.dt.bfloat16
ALU = mybir.AluOpType
AF = mybir.ActivationFunctionType
AX = mybir.AxisListType

ENGS = ("pe", "act", "dve", "pool", "sp")

D = 1024
NT = 2560
TL = 512
NTILE = 5
NBLK = 20
DFF = 2816
NF = 22
G = 2
NFG = NF // G
EPS = 1e-6
SLOT = 6144
NF32 = 8
NBF = 8


class Sched:
    def __init__(self):
        self.streams = {e: [] for e in ENGS}
        self.cnt = {}
        self.res = {}
        self.seen = {e: {} for e in ENGS}
        self.sem_names = []

    def _sem(self, name):
        if name not in self.cnt:
            self.cnt[name] = 0
            self.sem_names.append(name)
        return name

    def op(self, eng, fn, reads=(), writes=(), dma=None, ndma=1):
        own = self._sem("E_" + eng)
        need = {}

        def want(evt, is_reader):
            if evt is None:
                return
            s, v = evt
            if s == own and dma is None:
                if eng == "pe":
                    return
            if need.get(s, 0) < v:
                need[s] = v

        for k in reads:
            r = self.res.get(k)
            if r is not None:
                want(r[0], False)
                if isinstance(k, tuple) and k[0] == "ps":
                    for s, v in r[1].items():
                        if s != own:
                            want((s, v), True)
        for k in writes:
            r = self.res.get(k)
            if r is not None:
                want(r[0], False)
                for s, v in r[1].items():
                    want((s, v), True)
        waits = []
        seen = self.seen[eng]
        for s, v in need.items():
            if seen.get(s, 0) < v:
                seen[s] = v
                waits.append((s, v))
        if dma is None:
            self.cnt[own] += 1
            evt = (own, self.cnt[own])
            self.streams[eng].append((waits, fn, ("c", own, 1)))
        else:
            self._sem(dma)
            self.cnt[dma] += 16 * ndma
            evt = (dma, self.cnt[dma])
            self.streams[eng].append((waits, fn, ("d", dma, ndma)))
        for k in reads:
            r = self.res.setdefault(k, [None, {}])
            if r[1].get(evt[0], 0) < evt[1]:
                r[1][evt[0]] = evt[1]
        for k in writes:
            self.res[k] = [evt, {}]
        return evt

    def emit(self, nc, st):
        H = {}
        for name in self.sem_names:
            H[name] = st.enter_context(nc.semaphore(name))
        block = st.enter_context(nc.Block())
        final = [(s, v) for s, v in self.cnt.items() if v > 0]

        def make(eng):
            stream = self.streams[eng]

            def body(e):
                for waits, fn, inc in stream:
                    for s, v in waits:
                        e.wait_ge(H[s], v)
                    r = fn(e)
                    if inc[0] == "c":
                        r.then_inc(H[inc[1]], 1)
                    else:
                        assert len(r) == inc[2], (len(r), inc[2])
                        for ins in r:
                            ins.then_inc(H[inc[1]], 16)
                if eng == "sp":
                    for s, v in final:
                        e.wait_ge(H[s], v)
            return body

        block.tensor(make("pe"))
        block.scalar(make("act"))
        block.vector(make("dve"))
        block.gpsimd(make("pool"))
        block.sync(make("sp"))


class TPool:
    def __init__(self, tiles, prefix):
        self.tiles = tiles
        self.prefix = prefix
        self.i = 0

    def get(self):
        n = len(self.tiles)
        j = self.i % n
        self.i += 1
        return self.tiles[j], (self.prefix, j)


SP_CT = 0
SP_BT = SP_CT + 16
SP_GT = SP_BT + 144
SP_CW = SP_GT + 48
SP_QKG = SP_CW + 12
SP_SUB = SP_QKG + 4
SP_LAM = SP_SUB + 1
SP_SINK = SP_LAM + 256
SP_SINKP = SP_SINK + 16
NSP = SP_SINKP + 8

CB_BONES = 0
CB_ONES = 128
CB_ROPE = 256
CB_IDENT = 384
CB_MASK = 512
NCB = CB_MASK + 6 * 512

CF_IDENT = 0
NCF = 128


def build_nc():
    nc = bass.Bass("TRN2", target_bir_lowering=False)
    S = Sched()

    def din(name, shape):
        return nc.dram_tensor(name, list(shape), F32, kind="ExternalInput").ap()

    def dout(name, shape):
        return nc.dram_tensor(name, list(shape), F32, kind="ExternalOutput").ap()

    xin = din("xin", [NT, D])
    cek = din("cek", [512, 512])
    cev = din("cev", [512, 512])
    cok = din("cok", [512, 256])
    cov = din("cov", [512, 256])
    spar = din("spar", [128, NSP])
    cb_d = din("cb", [128, NCB])
    cf_d = din("cf", [128, NCF])
    cs_d = din("cs", [128, 4096])
    wmod0 = din("wmod0", [18, 128, 4096])
    wmod1 = din("wmod1", [72, 128, 1024])
    w13 = [din(f"w13_{k}", [D, 2 * DFF]) for k in range(4)]
    w2 = [din(f"w2_{k}", [DFF, D]) for k in range(4)]
    ewin = din("ewin", [D, 3072])
    ewout = din("ewout", [D, D])
    owin = din("owin", [D, 1536])
    owout = din("owout", [D, D])

    ys = dout("ys", [2048, D])
    yp = dout("yp", [512, D])
    nek = dout("nek", [512, 512])
    nev = dout("nev", [512, 512])
    nok = dout("nok", [512, 256])
    nov = dout("nov", [512, 256])

    st = ExitStack()

    def sbt(name, shape, dt):
        return st.enter_context(nc.sbuf_tensor(name, list(shape), dt))

    xT = sbt("xT", [128, 8, NT], F32)
    hT = sbt("hT", [128, 8, NT], BF16)
    wslot = [sbt(f"wslot{i}", [128, SLOT], BF16) for i in range(2)]
    spt = sbt("spt", [128, NSP], F32)
    cbt = sbt("cbt", [128, NCB], BF16)
    cft = sbt("cft", [128, NCF], F32)
    mT = sbt("mT", [128, 2, 72, 2], F32)
    AB = sbt("AB", [128, 2, 9, 8, 2], F32)
    scT = sbt("scT", [128, 8, 2], BF16)
    misc = sbt("misc", [128, 64], F32)
    big = sbt("big", [128, 7680], BF16)
    kT = big[:, 0:3072]
    v3 = big[:, 3072:7680].rearrange("p (b f) -> p b f", b=24)
    vsb = v3[:, :, 0:128]
    ubuf = big[:, 0:5128].bitcast(F32)
    mslot = [big[:, 5632 + i * 1024: 5632 + (i + 1) * 1024] for i in range(2)]
    onp = TPool([sbt(f"on_{i}", [128, TL], BF16) for i in range(2)], "o")
    actb = [[sbt(f"act{p}_{j}", [128, TL], BF16) for j in range(G)] for p in range(2)]
    qtp = TPool([sbt(f"qT_{i}", [128, TL], BF16) for i in range(2)], "q")
    rsp = TPool([sbt(f"rs_{i}", [128, TL], F32) for i in range(1)], "r")
    f32p = TPool([sbt(f"f32_{i}", [128, TL], F32) for i in range(NF32)], "f")
    bfp = TPool([sbt(f"bf_{i}", [128, TL], BF16) for i in range(NBF)], "b")
    ps = [st.enter_context(nc.psum_tensor(f"ps{i}", [128, TL], F32)) for i in range(8)]

    def PK(i):
        return ("ps", i)

    bones = cbt[:, CB_BONES:CB_BONES + 128]
    ones = cbt[:, CB_ONES:CB_ONES + 128]
    ropeR = cbt[:, CB_ROPE:CB_ROPE + 128]
    identb = cbt[:, CB_IDENT:CB_IDENT + 128]
    identf = cft[:, CF_IDENT:CF_IDENT + 128]

    def maskap(mi, n=TL):
        return cbt[:, CB_MASK + mi * 512: CB_MASK + mi * 512 + n]


    def mm(out, pairs, reads, writes, start=True, stop=True):
        def fn(e, out=out, pairs=pairs, start=start, stop=stop):
            n = len(pairs)
            r = None
            for i, (l, rr) in enumerate(pairs):
                r = e.matmul(out, lhsT=l, rhs=rr, start=(start and i == 0), stop=(stop and i == n - 1))
            return r
        S.op("pe", fn, reads=reads, writes=writes)

    def tr(out, in_, ident, reads, writes):
        S.op("pe", lambda e, out=out, in_=in_, ident=ident: e.transpose(out, in_, ident),
             reads=reads, writes=writes)

    def act(out, in_, func, reads, writes, bias=None, scale=None):
        kw = {}
        if bias is not None:
            kw["bias"] = bias
        if scale is not None:
            kw["scale"] = scale
        S.op("act", lambda e, out=out, in_=in_, func=func, kw=kw: e.activation(out=out, in_=in_, func=func, **kw),
             reads=reads, writes=writes)

    def tt(eng, out, in0, in1, op, reads, writes):
        S.op(eng, lambda e, out=out, in0=in0, in1=in1, op=op: e.tensor_tensor(out=out, in0=in0, in1=in1, op=op),
             reads=reads, writes=writes)

    def stt(eng, out, in0, scalar, in1, op0, op1, reads, writes):
        S.op(eng, lambda e, out=out, in0=in0, scalar=scalar, in1=in1, op0=op0, op1=op1:
             e.scalar_tensor_tensor(out=out, in0=in0, scalar=scalar, in1=in1, op0=op0, op1=op1),
             reads=reads, writes=writes)

    def ts(eng, out, in0, s1, s2, op0, op1, reads, writes):
        if s2 is None:
            S.op(eng, lambda e, out=out, in0=in0, s1=s1, op0=op0:
                 e.tensor_scalar(out=out, in0=in0, scalar1=s1, scalar2=None, op0=op0),
                 reads=reads, writes=writes)
        else:
            S.op(eng, lambda e, out=out, in0=in0, s1=s1, s2=s2, op0=op0, op1=op1:
                 e.tensor_scalar(out=out, in0=in0, scalar1=s1, scalar2=s2, op0=op0, op1=op1),
                 reads=reads, writes=writes)

    def cp(eng, out, in_, reads, writes):
        if eng == "act":
            S.op("act", lambda e, out=out, in_=in_: e.copy(out=out, in_=in_), reads=reads, writes=writes)
        else:
            S.op(eng, lambda e, out=out, in_=in_: e.tensor_copy(out=out, in_=in_), reads=reads, writes=writes)

    def recip(out, in_, reads, writes):
        S.op("dve", lambda e, out=out, in_=in_: e.reciprocal(out=out, in_=in_), reads=reads, writes=writes)

    def dma(eng, pairs, sem, reads, writes):
        def fn(e, pairs=pairs):
            return [e.dma_start(out=o, in_=i) for o, i in pairs]
        S.op(eng, fn, reads=reads, writes=writes, dma=sem, ndma=len(pairs))

    rr = {"i": 0}

    def alt():
        rr["i"] += 1
        return "act" if rr["i"] % 2 else "dve"

    def cond_of(t):
        return 1 if t == 4 else 0

    def xk(c, t):
        return ("x", c, t)

    def hk(c, t):
        return ("h", c, t)

    dma("sp", [(spt[:], spar), (cft[:], cf_d)], "D_const", [], ["spt", "cft"])
    dma("pool", [(cbt[:], cb_d)], "D_constb", [], ["cbt"])
    act(scT[:].rearrange("p c r -> p (c r)"), spt[:, SP_CT:SP_CT + 16], AF.Silu, ["spt"], ["scT"])

    units = []

    def WK(s):
        return ("w", s)

    def make_mod_unit(l, jg):
        def load(s):
            assert l == 0
            dma("pool", [(wslot[s][:, 0:4096], wmod0[jg])], f"D_w{s}", [], [WK(s)])

        def compute(s):
            ffn_flush()
            wv = wslot[s][:, 0:4096].rearrange("p (kc n) -> p kc n", kc=8)
            bank = 7
            for cc in range(4):
                idx = jg * 4 + cc
                mm(ps[bank][:, cc * 2:cc * 2 + 2],
                   [(wv[:, kc, cc * 128:(cc + 1) * 128], scT[:, kc, :]) for kc in range(8)],
                   [WK(s), "scT"], [PK(bank)])
            for r in range(2):
                src = ps[bank][:, 0:8].rearrange("p (a r) -> p a r", r=2)[:, :, r]
                tt("dve", mT[:, l, jg * 4:jg * 4 + 4, r], src,
                   spt[:, SP_BT + l * 72 + jg * 4: SP_BT + l * 72 + jg * 4 + 4], ALU.add,
                   [PK(bank), "spt"], [("mT", l)])
        return load, compute


    bg_tasks = []

    def make_small_mod_tasks(l):
        def load(k):
            sl = k % 2
            assert l == 1
            dma("pool", [(mslot[sl], wmod1[k])], f"D_m{sl}", ["BIG"], [("mw", sl)])

        def task(k):
            sl = k % 2
            wv = mslot[sl].rearrange("p (kc n) -> p kc n", kc=8)
            mm(ps[7][:, 8:10], [(wv[:, kc, :], scT[:, kc, :]) for kc in range(8)], [("mw", sl), "scT", "BIG"], [PK(7)])
            tt("dve", mT[:, l, k, :], ps[7][:, 8:10], spt[:, SP_BT + l * 72 + k: SP_BT + l * 72 + k + 1].to_broadcast([128, 2]),
               ALU.add, [PK(7), "spt"], [("mT", l)])
            if k + 2 < 72:
                load(k + 2)
        load(0)
        load(1)
        for k in range(72):
            bg_tasks.append(lambda k=k: task(k))

    def run_bg(n):
        for _ in range(n):
            if bg_tasks:
                bg_tasks.pop(0)()

    def derive_scalars(l):
        for i in range(3):
            g = spt[:, SP_GT + (l * 3 + i) * 8: SP_GT + (l * 3 + i) * 8 + 8]
            for r in range(2):
                sh = mT[:, l, (3 * i + 0) * 8:(3 * i + 0) * 8 + 8, r]
                sc = mT[:, l, (3 * i + 1) * 8:(3 * i + 1) * 8 + 8, r]
                gt = mT[:, l, (3 * i + 2) * 8:(3 * i + 2) * 8 + 8, r]
                stt("dve", AB[:, l, 3 * i + 0, :, r], sc, 1.0, g, ALU.add, ALU.mult,
                    [("mT", l), "spt"], [("AB", l)])
                cp("dve", AB[:, l, 3 * i + 1, :, r], sh, [("mT", l)], [("AB", l)])
                ts("dve", AB[:, l, 3 * i + 2, :, r], gt, 0.5 if i != 1 else 1.0, None, ALU.mult, None,
                   [("mT", l)], [("AB", l)])

    def ABs(l, kind, c, t):
        return AB[:, l, kind, c, cond_of(t):cond_of(t) + 1]

    def norm_stage(l, i):
        for t in range(NTILE):
            tsl = slice(t * TL, (t + 1) * TL)
            bank = 6
            for c in range(8):
                sq, sqk = bfp.get()
                if c % 2 == 0:
                    act(sq[:], xT[:, c, tsl], AF.Square, [xk(c, t)], [sqk])
                else:
                    tt("dve", sq[:], xT[:, c, tsl], xT[:, c, tsl], ALU.mult, [xk(c, t)], [sqk])
                mm(ps[bank][:], [(ones, sq[:])], [sqk, "cbt"], [PK(bank)], start=(c == 0), stop=(c == 7))
            rs, rsk = rsp.get()
            act(rs[:], ps[bank][:], AF.Ln, [PK(bank)], [rsk], bias=EPS, scale=1.0 / D)
            act(rs[:], rs[:], AF.Exp, [rsk], [rsk], scale=-0.5)
            for c in range(8):
                tmp, tk = f32p.get()
                stt("dve", tmp[:], xT[:, c, tsl], ABs(l, 3 * i, c, t), rs[:], ALU.mult, ALU.mult,
                    [xk(c, t), rsk, ("AB", l)], [tk])
                act(hT[:, c, tsl], tmp[:], AF.Identity, [tk, ("AB", l)], [hk(c, t)],
                    bias=ABs(l, 3 * i + 1, c, t), scale=1.0)

    ffn_state = {"jc": 0, "dcc": 0, "pending": None}
    att_state = {"c": 0}
    conv_pend = {"v": None}

    def make_ffn_unit(l, i, g):
        k = l * 2 + i
        f0 = g * G
        gk = 3 * (2 * i) + 2
        n13 = 8 * 2 * G * 128

        def load(s):
            wv = wslot[s][:, 0:n13].rearrange("p (kc gu n) -> p kc gu n", kc=8, gu=2)
            pairs = []
            for gu in range(2):
                src = w13[k][:, gu * DFF + f0 * 128: gu * DFF + (f0 + G) * 128].rearrange("(kc p) n -> p kc n", p=128)
                pairs.append((wv[:, :, gu, :], src))
            w2v = wslot[s][:, n13:n13 + G * 1024].rearrange("p (j d) -> p j d", j=G)
            pairs.append((w2v, w2[k][f0 * 128:(f0 + G) * 128, :].rearrange("(j p) d -> p j d", p=128)))
            dma("pool", pairs, f"D_w{s}", [], [WK(s)])

        def compute(s):
            wv = wslot[s][:, 0:n13].rearrange("p (kc gu n) -> p kc gu n", kc=8, gu=2)
            w2v = wslot[s][:, n13:n13 + G * 1024].rearrange("p (j d) -> p j d", j=G)
            for t in range(NTILE):
                tsl = slice(t * TL, (t + 1) * TL)
                par = (g * NTILE + t) % 2
                for j in range(G):
                    jc = ffn_state["jc"]
                    ffn_state["jc"] += 1
                    bg_, bu_ = (0, 1) if jc % 2 == 0 else (2, 3)
                    mm(ps[bg_][:], [(wv[:, kc, 0, j * 128:(j + 1) * 128], hT[:, kc, tsl]) for kc in range(8)],
                       [WK(s)] + [hk(c, t) for c in range(8)], [PK(bg_)])
                    mm(ps[bu_][:], [(wv[:, kc, 1, j * 128:(j + 1) * 128], hT[:, kc, tsl]) for kc in range(8)],
                       [WK(s)] + [hk(c, t) for c in range(8)], [PK(bu_)])
                    sg, sgk = f32p.get()
                    act(sg[:], ps[bg_][:], AF.Silu, [PK(bg_)], [sgk])
                    tt("dve", actb[par][j][:], sg[:], ps[bu_][:], ALU.mult, [sgk, PK(bu_)], [("act", par, j)])
                prev = ffn_state["pending"]
                if prev is not None:
                    prev()

                def w2step(t=t, tsl=tsl, par=par, s=s, w2v=w2v):
                    for dc in range(8):
                        dcc = ffn_state["dcc"]
                        ffn_state["dcc"] += 1
                        bk = 4 + dcc % 4
                        mm(ps[bk][:], [(w2v[:, j, dc * 128:(dc + 1) * 128], actb[par][j][:]) for j in range(G)],
                           [WK(s)] + [("act", par, j) for j in range(G)], [PK(bk)])
                        stt("dve", xT[:, dc, tsl], ps[bk][:], ABs(l, gk, dc, t), xT[:, dc, tsl], ALU.mult, ALU.add,
                            [PK(bk), xk(dc, t), ("AB", l)], [xk(dc, t)])
                ffn_state["pending"] = w2step
        return load, compute

    def ffn_flush():
        prev = ffn_state["pending"]
        if prev is not None:
            prev()
        ffn_state["pending"] = None

    def wout_accum(l, wrows, on, onk, t, wkey, banks=(2, 3)):
        tsl = slice(t * TL, (t + 1) * TL)
        for dc in range(8):
            bk = banks[dc % len(banks)]
            mm(ps[bk][:], [(wrows[:, dc * 128:(dc + 1) * 128], on)], [wkey, onk], [PK(bk)])
            stt("dve", xT[:, dc, tsl], ps[bk][:], ABs(l, 5, dc, t), xT[:, dc, tsl], ALU.mult, ALU.add,
                [PK(bk), xk(dc, t), ("AB", l)], [xk(dc, t)])

    def proj_norm_tasks(l, wcols, t, gcol, wkey, do_rope, dest, destk, banks=(0, 1, 1), out=None):
        pb, sb2, rb = banks
        tsl = slice(t * TL, (t + 1) * TL)
        stt_ = {}

        def t0():
            mm(ps[pb][:], [(wcols[:, kc, :], hT[:, kc, tsl]) for kc in range(8)],
               [wkey] + [hk(c, t) for c in range(8)], [PK(pb)])
            sq, sqk = bfp.get()
            act(sq[:], ps[pb][:], AF.Square, [PK(pb)], [sqk])
            stt_["sq"] = (sq, sqk)

        def t1():
            sq, sqk = stt_["sq"]
            mm(ps[sb2][:], [(bones, sq[:])], [sqk, "cbt"], [PK(sb2)])
            rs, rsk = f32p.get()
            act(rs[:], ps[sb2][:], AF.Ln, [PK(sb2)], [rsk], bias=EPS, scale=1.0 / 64)
            act(rs[:], rs[:], AF.Exp, [rsk], [rsk], scale=-0.5)
            kn, knk = f32p.get()
            stt("dve", kn[:], ps[pb][:], spt[:, SP_QKG + gcol:SP_QKG + gcol + 1], rs[:], ALU.mult, ALU.mult,
                [PK(pb), rsk, "spt"], [knk])
            stt_["kn"] = (kn, knk)
            if out is not None:
                out["kn"] = (kn, knk)
            if not do_rope:
                cp("act", dest, kn[:], [knk, "BIG"], [destk])
            else:
                knb, knbk = bfp.get()
                cp("act", knb[:], kn[:], [knk], [knbk])
                stt_["knb"] = (knb, knbk)
                cst, cstk = f32p.get()
                snt, sntk = f32p.get()
                dma("sp", [(cst[:], cs_d[:, t * TL:(t + 1) * TL])], "D_" + str(cstk), [], [cstk])
                dma("sp", [(snt[:], cs_d[:, 2048 + t * TL: 2048 + (t + 1) * TL])], "D_" + str(sntk), [], [sntk])
                stt_["cs"] = (cst, cstk, snt, sntk)

        def t2():
            kn, knk = stt_["kn"]
            knb, knbk = stt_["knb"]
            cst, cstk, snt, sntk = stt_["cs"]
            mm(ps[rb][:], [(ropeR, knb[:])], [knbk, "cbt"], [PK(rb)])
            tt("dve", cst[:], kn[:], cst[:], ALU.mult, [knk, cstk], [cstk])
            tt("dve", snt[:], ps[rb][:], snt[:], ALU.mult, [PK(rb), sntk], [sntk])
            tt("dve", dest, cst[:], snt[:], ALU.add, [cstk, sntk, "BIG"], [destk])

        return [t0, t1] + ([t2] if do_rope else [])

    def proj_norm(l, wcols, t, gcol, wkey, do_rope, dest=None, destk=None, banks=(0, 1, 1)):
        if dest is None:
            ob, obk = bfp.get()
            dest = ob[:]
            destk = obk
        out = {}
        for f in proj_norm_tasks(l, wcols, t, gcol, wkey, do_rope, dest, destk, banks=banks, out=out):
            f()
        kn, knk = out["kn"]
        return kn, knk, dest, destk


    def k_stage_pipelined(l, kcols, gcol, wkey, ctx_dst, col0, ncol):
        tl = []
        outs = []
        for t in range(NTILE):
            kd = kT[:, 512 + t * TL: 512 + (t + 1) * TL] if t < 4 else kT[:, 2560:3072]
            o = {}
            bk = (0, 1, 1) if t % 2 == 0 else (2, 3, 3)
            tl.append(proj_norm_tasks(l, kcols, t, gcol, wkey, t < 4, kd, ("kT", t), banks=bk, out=o))
            outs.append(o)
        for k in range(NTILE + 2):
            for d in range(3):
                t = k - d
                if 0 <= t < NTILE and d < len(tl[t]):
                    tl[t][d]()
                    if t == 4 and d == 1:
                        kn, knk = outs[4]["kn"]
                        store_ctx_k(kn, knk, ctx_dst, col0, ncol)

    def store_ctx_k(kn, knk, dst, col0, ncol):
        for blk in range(4):
            tr(ps[5][:, blk * 128:(blk + 1) * 128], kn[:, blk * 128:(blk + 1) * 128], identf, [knk, "cft"], [PK(5)])
        stg, stgk = f32p.get()
        cp("act", stg[:], ps[5][:], [PK(5)], [stgk])
        src = stg[:].rearrange("p (b f) -> p b f", b=4)[:, :, 0:ncol]
        d = dst[:, col0:col0 + ncol].rearrange("(b p) f -> p b f", p=128)
        dma("sp", [(d, src)], "D_" + str(stgk), [stgk], [])

    def load_cache_kT(src_ap, ndup, sem):
        stg, stgk = f32p.get()
        sv = stg[:].rearrange("p (b f) -> p b f", b=4)
        pairs = []
        w = 128 // ndup
        for dd in range(ndup):
            pairs.append((sv[:, :, dd * w:(dd + 1) * w], src_ap.rearrange("(b p) f -> p b f", p=128)))
        dma("sp", pairs, "D_" + str(stgk), [], [stgk])
        for blk in range(4):
            tr(ps[4][:, blk * 128:(blk + 1) * 128], stg[:, blk * 128:(blk + 1) * 128], identf, [stgk, "cft"], [PK(4)])
        cp("dve", kT[:, 0:512], ps[4][:], [PK(4), "BIG"], ["kT_cache"])

    def v_stage(wv_cols, wkey, ctx_dst, col0, ncol, odd=False):
        for bg4 in range(5):
            bank = 2 + bg4 % 4
            for b4 in range(4):
                blk = bg4 * 4 + b4
                mm(ps[bank][:, b4 * 128:(b4 + 1) * 128],
                   [(hT[:, kc, blk * 128:(blk + 1) * 128], wv_cols[:, kc, :]) for kc in range(8)],
                   [wkey] + [hk(c, bg4) for c in range(8)], [PK(bank)])
            pv = ps[bank][:].rearrange("p (b f) -> p b f", b=4)
            b0 = 4 + bg4 * 4
            if not odd:
                cp(alt(), v3[:, b0:b0 + 4, 0:128], pv, [PK(bank), "BIG"], [("v", bg4)])
            else:
                cp("act", v3[:, b0:b0 + 4, 0:64], pv[:, :, 0:64], [PK(bank), "BIG"], [("v", bg4)])
                cp("dve", v3[:, b0:b0 + 4, 128:192], pv[:, :, 64:128], [PK(bank), "BIG", ("v", bg4)], [("v", bg4)])
            if bg4 == 4:
                stg, stgk = f32p.get()
                cp("act", stg[:], ps[bank][:], [PK(bank)], [stgk])
                src = stg[:].rearrange("p (b f) -> p b f", b=4)[:, :, 0:ncol]
                d = ctx_dst[:, col0:col0 + ncol].rearrange("(b p) f -> p b f", p=128)
                dma("sp", [(d, src)], "D_" + str(stgk), [stgk], [])

    L0 = 0

    def make_even_attn_unit(h):
        def load(s):
            wv = wslot[s][:, 0:3072].rearrange("p (kc a n) -> p kc a n", kc=8, a=3)
            pairs = []
            for a in range(3):
                src = ewin[:, a * 512 + h * 128: a * 512 + (h + 1) * 128].rearrange("(kc p) n -> p kc n", p=128)
                pairs.append((wv[:, :, a, :], src))
            pairs.append((wslot[s][:, 3072:4096], ewout[h * 128:(h + 1) * 128, :]))
            dma("pool", pairs, f"D_w{s}", [], [WK(s)])

        def compute(s):
            l = L0
            wv = wslot[s][:, 0:3072].rearrange("p (kc a n) -> p kc a n", kc=8, a=3)
            wrows = wslot[s][:, 3072:4096]
            load_cache_kT(cek[:, h * 128:(h + 1) * 128], 1, None)
            dma("pool", [(v3[:, 0:4, 0:128], cev[:, h * 128:(h + 1) * 128].rearrange("(b p) f -> p b f", p=128))],
                "D_vc", ["BIG"], [("v", "c")])
            k_stage_pipelined(l, wv[:, :, 1, :], 1, WK(s), nek, h * 128, 128)
            import os
            _dbg = int(os.environ.get("KDBG", "99"))
            if h >= 1 and _dbg <= 1:
                return
            v_stage(wv[:, :, 2, :], WK(s), nev, h * 128, 128)
            if h >= 1 and _dbg <= 2:
                return
            def qchain(t_):
                qt_, qtk_ = qtp.get()
                return (proj_norm_tasks(l, wv[:, :, 0, :], t_, 0, WK(s), t_ < 4, qt_[:], qtk_, banks=(0, 1, 1)), qt_[:], qtk_)

            tasks, qT, qTk = qchain(0)
            for f in tasks:
                f()
            for t in range(NTILE):
                if t < 4:
                    segs = [(0, TL, [(kc * 128, kc, ["kT_cache", ("v", "c")] if kc < 4 else
                                      [("kT", (kc - 4) // 4), ("v", (kc - 4) // 4)]) for kc in range(20)])]
                else:
                    segs = [(sq_ * 256, 256, [(2560 + sq_ * 256 + k2 * 128, 20 + sq_ * 2 + k2, [("kT", 4), ("v", 4)])
                                              for k2 in range(2)]) for sq_ in range(2)]
                for q0, qn, chunks in segs:
                    nck = len(chunks)
                    pend = None
                    for ci in range(nck + 1):
                        cur = None
                        if ci < nck:
                            kc0, vb, rkeys = chunks[ci]
                            sb = (att_state["c"] % 2) * 2
                            att_state["c"] += 1
                            mm(ps[sb][:, 0:qn], [(kT[0:64, kc0:kc0 + 128], qT[0:64, q0:q0 + qn])], [qTk] + rkeys, [PK(sb)])
                            mm(ps[sb + 1][:, 0:qn], [(kT[64:128, kc0:kc0 + 128], qT[64:128, q0:q0 + qn])], [qTk] + rkeys, [PK(sb + 1)])
                            p1, p1k = bfp.get()
                            act(p1[:, 0:qn], ps[sb][:, 0:qn], AF.Exp, [PK(sb)], [p1k], scale=0.125)
                            p2, p2k = bfp.get()
                            act(p2[:, 0:qn], ps[sb + 1][:, 0:qn], AF.Exp, [PK(sb + 1)], [p2k], scale=0.125)
                            cur = (p1, p1k, p2, p2k, vb, rkeys, ci)
                        if pend is not None:
                            p1, p1k, p2, p2k, vb, rkeys, cj = pend
                            st_, sp_ = (cj == 0), (cj == nck - 1)
                            mm(ps[4][:, q0:q0 + qn], [(vsb[:, vb, :], p1[:, 0:qn])], [p1k] + rkeys, [PK(4)], st_, sp_)
                            mm(ps[6][:, q0:q0 + qn], [(ones, p1[:, 0:qn])], [p1k, "cbt"], [PK(6)], st_, sp_)
                            mm(ps[5][:, q0:q0 + qn], [(vsb[:, vb, :], p2[:, 0:qn])], [p2k] + rkeys, [PK(5)], st_, sp_)
                            mm(ps[7][:, q0:q0 + qn], [(ones, p2[:, 0:qn])], [p2k, "cbt"], [PK(7)], st_, sp_)
                        pend = cur
                ntasks = []
                if t + 1 < NTILE:
                    ntasks, nqT, nqTk = qchain(t + 1)
                if ntasks:
                    ntasks.pop(0)()
                r1, r1k = f32p.get()
                recip(r1[:], ps[6][:], [PK(6)], [r1k])
                a1, a1k = f32p.get()
                tt("dve", a1[:], ps[4][:], r1[:], ALU.mult, [PK(4), r1k], [a1k])
                r2, r2k = f32p.get()
                recip(r2[:], ps[7][:], [PK(7)], [r2k])
                a2, a2k = f32p.get()
                tt("dve", a2[:], ps[5][:], r2[:], ALU.mult, [PK(5), r2k], [a2k])
                stt("dve", a1[:], a2[:], misc[:, 0:1], a1[:], ALU.mult, ALU.add, [a2k, a1k, "misc"], [a1k])
                if ntasks:
                    ntasks.pop(0)()
                sq, sqk = bfp.get()
                act(sq[:], a1[:], AF.Square, [a1k], [sqk])
                mm(ps[2][:], [(ones, sq[:])], [sqk, "cbt"], [PK(2)])
                rs, rsk = f32p.get()
                act(rs[:], ps[2][:], AF.Ln, [PK(2)], [rsk], bias=EPS, scale=1.0 / 128)
                act(rs[:], rs[:], AF.Exp, [rsk], [rsk], scale=-0.5)
                on, onk = bfp.get()
                stt("dve", on[:], a1[:], misc[:, 1:2], rs[:], ALU.mult, ALU.mult, [a1k, rsk, "misc"], [onk])
                while ntasks:
                    ntasks.pop(0)()
                wout_accum(l, wrows, on[:], onk, t, WK(s), banks=(4, 5, 6, 7))
                if t + 1 < NTILE:
                    qT, qTk = nqT, nqTk
        return load, compute

    def useg(t):
        if t < 4:
            return [(0, TL, t * TL + 1)]
        return [(0, 256, 2048 + 2), (256, 256, 2304 + 3)]

    def make_even_conv_unit(j):
        def load(s):
            wv = wslot[s][:, 0:3072].rearrange("p (kc a n) -> p kc a n", kc=8, a=3)
            pairs = []
            for a in range(3):
                src = ewin[:, 1536 + a * 512 + j * 128: 1536 + a * 512 + (j + 1) * 128].rearrange("(kc p) n -> p kc n", p=128)
                pairs.append((wv[:, :, a, :], src))
            pairs.append((wslot[s][:, 3072:4096], ewout[512 + j * 128: 512 + (j + 1) * 128, :]))
            dma("pool", pairs, f"D_w{s}", [], [WK(s)])

        def compute(s):
            l = L0
            wv = wslot[s][:, 0:3072].rearrange("p (kc a n) -> p kc a n", kc=8, a=3)
            wrows = wslot[s][:, 3072:4096]
            for t in range(NTILE):
                tsl = slice(t * TL, (t + 1) * TL)
                rd = [WK(s)] + [hk(c, t) for c in range(8)]
                b0_ = (t % 2) * 2
                mm(ps[b0_][:], [(wv[:, kc, 1, :], hT[:, kc, tsl]) for kc in range(8)], rd, [PK(b0_)])
                mm(ps[b0_ + 1][:], [(wv[:, kc, 2, :], hT[:, kc, tsl]) for kc in range(8)], rd, [PK(b0_ + 1)])
                xs, xsk = f32p.get()
                cp("act", xs[:], ps[b0_ + 1][:], [PK(b0_ + 1)], [xsk])
                for (o0, n, uc) in useg(t):
                    tt("dve", ubuf[:, uc:uc + n], ps[b0_][:, o0:o0 + n], xs[:, o0:o0 + n], ALU.mult,
                       [PK(b0_), xsk, "upad", "BIG"], [("u", t)])
                run_bg(2)
            for t in range(NTILE):
                tsl = slice(t * TL, (t + 1) * TL)
                rd = [WK(s)] + [hk(c, t) for c in range(8)]
                bk = t % 3
                mm(ps[bk][:], [(wv[:, kc, 0, :], hT[:, kc, tsl]) for kc in range(8)], rd, [PK(bk)])
                acc, acck = f32p.get()
                uk = [("u", tt_) for tt_ in range(NTILE)] + ["upad", "BIG"]
                for (o0, n, uc) in useg(t):
                    ts("dve", acc[:, o0:o0 + n], ubuf[:, uc - 1:uc - 1 + n], spt[:, SP_CW + j * 3: SP_CW + j * 3 + 1], None,
                       ALU.mult, None, uk + ["spt"], [acck])
                    stt("dve", acc[:, o0:o0 + n], ubuf[:, uc:uc + n], spt[:, SP_CW + j * 3 + 1: SP_CW + j * 3 + 2],
                        acc[:, o0:o0 + n], ALU.mult, ALU.add, uk + [acck, "spt"], [acck])
                    stt("dve", acc[:, o0:o0 + n], ubuf[:, uc + 1:uc + 1 + n], spt[:, SP_CW + j * 3 + 2: SP_CW + j * 3 + 3],
                        acc[:, o0:o0 + n], ALU.mult, ALU.add, uk + [acck, "spt"], [acck])
                y, yk = bfp.get()
                tt("dve", y[:], ps[bk][:], acc[:], ALU.mult, [PK(bk), acck], [yk])
                if conv_pend["v"] is not None:
                    y_, yk_, t_ = conv_pend["v"]
                    wout_accum(l, wrows, y_[:], yk_, t_, WK(s), banks=(3, 4, 5, 6))
                conv_pend["v"] = (y, yk, t)
                run_bg(2)
            y_, yk_, t_ = conv_pend["v"]
            wout_accum(l, wrows, y_[:], yk_, t_, WK(s), banks=(3, 4, 5, 6))
            conv_pend["v"] = None
        return load, compute

    L1 = 1

    def make_odd_unit(g):
        def load(s):
            qv = wslot[s][:, 0:2048].rearrange("p (kc n) -> p kc n", kc=8)
            kv = wslot[s][:, 2048:3072].rearrange("p (kc n) -> p kc n", kc=8)
            vv = wslot[s][:, 3072:4096].rearrange("p (kc n) -> p kc n", kc=8)
            ov = wslot[s][:, 4096:6144].rearrange("p (j d) -> p j d", j=2)
            pairs = [(qv, owin[:, g * 256:(g + 1) * 256].rearrange("(kc p) n -> p kc n", p=128))]
            for dd in range(2):
                pairs.append((kv[:, :, dd * 64:(dd + 1) * 64],
                              owin[:, 1024 + g * 64: 1024 + (g + 1) * 64].rearrange("(kc p) n -> p kc n", p=128)))
                pairs.append((vv[:, :, dd * 64:(dd + 1) * 64],
                              owin[:, 1280 + g * 64: 1280 + (g + 1) * 64].rearrange("(kc p) n -> p kc n", p=128)))
            pairs.append((ov, owout[g * 256:(g + 1) * 256, :].rearrange("(j p) d -> p j d", p=128)))
            dma("pool", pairs, f"D_w{s}", [], [WK(s)])

        def compute(s):
            l = L1
            qv = wslot[s][:, 0:2048].rearrange("p (kc n) -> p kc n", kc=8)
            kv = wslot[s][:, 2048:3072].rearrange("p (kc n) -> p kc n", kc=8)
            vv = wslot[s][:, 3072:4096].rearrange("p (kc n) -> p kc n", kc=8)
            ov = wslot[s][:, 4096:6144].rearrange("p (j d) -> p j d", j=2)
            BK = (6, 7, 6)
            if g == 0:
                S.op("pool", lambda e: e.memset(v3[:, :, 64:128], 1.0), reads=["BIG"],
                     writes=[("v", i) for i in range(5)] + [("v", "c")])
            load_cache_kT(cok[:, g * 64:(g + 1) * 64], 2, None)
            vsrc = cov[:, g * 64:(g + 1) * 64].rearrange("(b p) f -> p b f", p=128)
            dma("pool", [(v3[:, 0:4, 0:64], vsrc), (v3[:, 0:4, 128:192], vsrc)], "D_vc", ["BIG"], [("v", "c")])
            k_stage_pipelined(l, kv, 3, WK(s), nok, g * 64, 64)
            v_stage(vv, WK(s), nov, g * 64, 64, odd=True)

            hooks = []

            def run_hooks(n):
                for _ in range(n):
                    if hooks:
                        hooks.pop(0)()

            def qchain(i):
                pr_, t_ = divmod(i, NTILE)
                qt_, qtk_ = qtp.get()
                tasks = proj_norm_tasks(l, qv[:, :, pr_ * 128:(pr_ + 1) * 128], t_, 2, WK(s), t_ < 4,
                                        qt_[:], qtk_, banks=BK)
                return tasks, qt_[:], qtk_

            wtog = {"v": 0}

            def wout_task(pr_, t_, dc, on, onk):
                tsl = slice(t_ * TL, (t_ + 1) * TL)
                if any(getattr(h_, "is_chain", False) for h_ in hooks):
                    bk = 7
                else:
                    wtog["v"] ^= 1
                    bk = 6 + wtog["v"]
                mm(ps[bk][:], [(ov[:, pr_, dc * 128:(dc + 1) * 128], on)], [WK(s), onk], [PK(bk)])
                stt("dve", xT[:, dc, tsl], ps[bk][:], ABs(l, 5, dc, t_), xT[:, dc, tsl], ALU.mult, ALU.add,
                    [PK(bk), xk(dc, t_), ("AB", l)], [xk(dc, t_)])

            tasks, qT, qTk = qchain(0)
            for f in tasks:
                f()
            NI = 2 * NTILE
            for i in range(NI):
                pr, t = divmod(i, NTILE)
                pairidx = g * 2 + pr
                if i + 1 < NI:
                    ntasks, nqT, nqTk = qchain(i + 1)
                    for f_ in ntasks:
                        f_.is_chain = True
                    old = list(hooks)
                    del hooks[:]
                    while ntasks or old:
                        if ntasks:
                            hooks.append(ntasks.pop(0))
                        if old:
                            hooks.append(old.pop(0))
                if t < 4:
                    chunks = [(kc * 128, kc, ["kT_cache", ("v", "c")], None) for kc in range(4)]
                    for jb in range(max(0, 4 * t - 1), min(15, 4 * t + 4) + 1):
                        chunks.append((512 + jb * 128, 4 + jb, [("kT", jb // 4), ("v", jb // 4)], jb - 4 * t + 1))
                    segs = [(0, TL, chunks)]
                else:
                    segs = [(sq_ * 256, 256, [(2560 + sq_ * 256 + k2 * 128, 20 + sq_ * 2 + k2, [("kT", 4), ("v", 4)], None)
                                              for k2 in range(2)]) for sq_ in range(2)]
                for q0, qn, chunks in segs:
                    nck = len(chunks)
                    pend = None
                    for ci in range(nck + 1):
                        cur = None
                        if ci < nck:
                            kc0, vb, rkeys, mi = chunks[ci]
                            sb = (att_state["c"] % 2) * 2
                            att_state["c"] += 1
                            if mi is None:
                                c0, c1 = q0, q0 + qn
                            else:
                                jrel = mi - 1
                                c0, c1 = max(0, jrel - 1) * 128, (min(3, jrel + 1) + 1) * 128
                            cw = c1 - c0
                            mm(ps[sb][:, 0:cw], [(kT[0:64, kc0:kc0 + 128], qT[0:64, c0:c1])], [qTk] + rkeys, [PK(sb)])
                            mm(ps[sb + 1][:, 0:cw], [(kT[64:128, kc0:kc0 + 128], qT[64:128, c0:c1])], [qTk] + rkeys, [PK(sb + 1)])
                            p1, p1k = bfp.get()
                            act(p1[:, 0:cw], ps[sb][:, 0:cw], AF.Exp, [PK(sb)], [p1k], scale=0.125)
                            p2, p2k = bfp.get()
                            act(p2[:, 0:cw], ps[sb + 1][:, 0:cw], AF.Exp, [PK(sb + 1)], [p2k], scale=0.125)
                            if mi is not None:
                                for qb in range(c0 // 128, c1 // 128):
                                    if qb == jrel + 1:
                                        tri = cbt[:, CB_MASK: CB_MASK + 128]
                                    elif qb == jrel - 1:
                                        tri = cbt[:, CB_MASK + 5 * 512 + 384: CB_MASK + 6 * 512]
                                    else:
                                        continue
                                    o_ = qb * 128 - c0
                                    tt("dve", p1[:, o_:o_ + 128], p1[:, o_:o_ + 128], tri, ALU.mult, [p1k, "cbt"], [p1k])
                                    tt("dve", p2[:, o_:o_ + 128], p2[:, o_:o_ + 128], tri, ALU.mult, [p2k, "cbt"], [p2k])
                            cur = (p1, p1k, p2, p2k, vb, rkeys, ci, c0, c1)
                        if pend is not None:
                            p1, p1k, p2, p2k, vb, rkeys, cj, c0, c1 = pend
                            cw = c1 - c0
                            st_, sp_ = (cj == 0), (cj == nck - 1)
                            mm(ps[4][:, c0:c1], [(v3[:, vb, 0:128], p1[:, 0:cw])], [p1k] + rkeys, [PK(4)], st_, sp_)
                            mm(ps[5][:, c0:c1], [(v3[:, vb, 64:192], p2[:, 0:cw])], [p2k] + rkeys, [PK(5)], st_, sp_)
                        pend = cur
                        run_hooks(2)
                run_hooks(1000)
                zc, zck = f32p.get()
                cp("dve", zc[0:64, :], ps[4][64:128, :], [PK(4)], [zck])
                cp("dve", zc[64:128, :], ps[5][0:64, :], [PK(5), zck], [zck])
                ts("dve", zc[:], zc[:], misc[:, 24 + pairidx:25 + pairidx], None, ALU.add, None, [zck, "misc"], [zck])
                recip(zc[:], zc[:], [zck], [zck])
                on, onk = onp.get()
                tt("dve", on[0:64, :], ps[4][0:64, :], zc[0:64, :], ALU.mult, [PK(4), zck], [onk])
                tt("dve", on[64:128, :], ps[5][64:128, :], zc[64:128, :], ALU.mult, [PK(5), zck, onk], [onk])
                for dc in range(8):
                    hooks.append(lambda pr=pr, t=t, dc=dc, on=on, onk=onk: wout_task(pr, t, dc, on[:], onk))
                if i + 1 < NI:
                    qT, qTk = nqT, nqTk
            run_hooks(1000)
        return load, compute

    def plain(fn):
        return (None, lambda s, fn=fn: fn())

    def load_x():
        for b in range(NBLK):
            t = b // 4
            for half in range(2):
                stg, stgk = f32p.get()
                dma("sp", [(stg[:], xin[b * 128:(b + 1) * 128, half * 512:(half + 1) * 512])], "D_" + str(stgk), [], [stgk])
                bank = (b * 2 + half) % 4
                for j in range(4):
                    tr(ps[bank][:, j * 128:(j + 1) * 128], stg[:, j * 128:(j + 1) * 128], identf, [stgk, "cft"], [PK(bank)])
                cp(alt(), xT[:, half * 4:half * 4 + 4, b * 128:(b + 1) * 128],
                   ps[bank][:].rearrange("p (j n) -> p j n", j=4), [PK(bank)],
                   [xk(c, t) for c in range(half * 4, half * 4 + 4)])

    def store_y():
        for b in range(NBLK):
            t = b // 4
            for half in range(2):
                bank = (b * 2 + half) % 4
                for j in range(4):
                    c = half * 4 + j
                    tr(ps[bank][:, j * 128:(j + 1) * 128], xT[:, c, b * 128:(b + 1) * 128], identf,
                       [xk(c, t), "cft"], [PK(bank)])
                stg, stgk = f32p.get()
                cp(alt(), stg[:], ps[bank][:], [PK(bank)], [stgk])
                if b < 16:
                    d = ys[b * 128:(b + 1) * 128, half * 512:(half + 1) * 512]
                else:
                    d = yp[(b - 16) * 128:(b - 15) * 128, half * 512:(half + 1) * 512]
                dma("sp", [(d, stg[:])], "D_" + str(stgk), [stgk], [])

    def setup_misc():
        lam_init = 0.8 - 0.6 * float(np.exp(-0.3 * 0))
        lm = spt[:, SP_LAM:SP_LAM + 256]
        pr_, prk = f32p.get()
        tt("dve", pr_[:, 0:64], lm[:, 0:64], lm[:, 64:128], ALU.mult, ["spt"], [prk])
        tt("dve", pr_[:, 64:128], lm[:, 128:192], lm[:, 192:256], ALU.mult, ["spt", prk], [prk])
        S.op("dve", lambda e: e.reduce_sum(out=misc[:, 2:3], in_=pr_[:, 0:64], axis=AX.X), reads=[prk], writes=["m2"])
        S.op("dve", lambda e: e.reduce_sum(out=misc[:, 3:4], in_=pr_[:, 64:128], axis=AX.X), reads=[prk], writes=["m3"])
        act(misc[:, 4:6], misc[:, 2:4], AF.Exp, ["m2", "m3"], ["m45"])
        tt("dve", misc[:, 6:7], misc[:, 5:6], misc[:, 4:5], ALU.subtract, ["m45"], ["m6"])
        ts("dve", misc[:, 0:1], misc[:, 6:7], -lam_init, None, ALU.add, None, ["m6"], ["misc0"])
        ts("dve", misc[:, 1:2], spt[:, SP_SUB:SP_SUB + 1], 1.0 - lam_init, None, ALU.mult, None, ["spt"], ["misc1"])
        act(misc[:, 8:32], spt[:, SP_SINK:SP_SINK + 24], AF.Exp, ["spt"], ["misc8"])
        S.op("pool", lambda e: e.memset(ubuf[:, 0:1], 0.0), writes=["up0"])
        S.op("pool", lambda e: e.memset(ubuf[:, 2049:2050], 0.0), writes=["up1"])
        S.op("pool", lambda e: e.memset(ubuf[:, 2306:2307], 0.0), writes=["up2"])
        S.op("pool", lambda e: e.memset(ubuf[:, 2563:2564], 0.0), writes=["up3"])
        S.op("dve", lambda e: e.memset(misc[:, 34:35], 0.0),
             reads=["misc0", "misc1", "misc8", "up0", "up1", "up2", "up3"], writes=["misc", "upad", "BIG"])

    seq = []
    seq.append(plain(load_x))
    for jg in range(18):
        seq.append(make_mod_unit(0, jg))
    seq.append(plain(lambda: derive_scalars(0)))
    seq.append(plain(setup_misc))
    seq.append(plain(lambda: norm_stage(0, 0)))
    for g in range(NFG):
        seq.append(make_ffn_unit(0, 0, g))
    seq.append(plain(ffn_flush))
    seq.append(plain(lambda: norm_stage(0, 1)))
    seq.append(plain(lambda: make_small_mod_tasks(1)))
    for j in range(4):
        seq.append(make_even_conv_unit(j))
    seq.append(plain(lambda: run_bg(1000)))
    seq.append(plain(lambda: S.op("dve", lambda e: e.memset(misc[:, 35:36], 0.0), reads=[], writes=["BIG"])))
    for h in range(4):
        seq.append(make_even_attn_unit(h))
    seq.append(plain(lambda: norm_stage(0, 2)))
    for g in range(NFG):
        seq.append(make_ffn_unit(0, 1, g))
    seq.append(plain(ffn_flush))
    seq.append(plain(lambda: derive_scalars(1)))
    seq.append(plain(lambda: norm_stage(1, 0)))
    for g in range(NFG):
        seq.append(make_ffn_unit(1, 0, g))
    seq.append(plain(ffn_flush))
    seq.append(plain(lambda: norm_stage(1, 1)))
    for g in range(4):
        seq.append(make_odd_unit(g))
    seq.append(plain(lambda: norm_stage(1, 2)))
    for g in range(NFG):
        seq.append(make_ffn_unit(1, 1, g))
    seq.append(plain(ffn_flush))
    seq.append(plain(store_y))

    widx = [i for i, u in enumerate(seq) if u[0] is not None]
    slot_of = {i: n % 2 for n, i in enumerate(widx)}
    nxt = {widx[n]: widx[n + 1] for n in range(len(widx) - 1)}
    if widx:
        seq[widx[0]][0](slot_of[widx[0]])
    import os
    _lim = int(os.environ.get("KSEQ_LIMIT", "100000"))
    for i, (ld, comp) in enumerate(seq):
        if i >= _lim:
            break
        if ld is not None:
            comp(slot_of[i])
            if i in nxt:
                seq[nxt[i]][0](slot_of[nxt[i]])
        else:
            comp(None)
    for _ in range(int(os.environ.get("KPAD", "0"))):
        S.op("dve", lambda e: e.memset(misc[:, 40:41], 0.0), reads=[], writes=["pad"])
        S.op("act", lambda e: e.copy(out=misc[:, 42:43], in_=misc[:, 40:41]), reads=["pad"], writes=["pad2"])
    if os.environ.get("KSEQ_PRINT"):
        print("SEM COUNTS", S.cnt)
        print("OPS", {e: len(v) for e, v in S.streams.items()}, "WAITS", {e: sum(len(w[0]) for w in v) for e, v in S.streams.items()})

    S.emit(nc, st)
    st.close()
    return nc


def _consts():
    cb = np.zeros((128, NCB), np.float32)
    p = np.arange(128)
    cb[:, CB_BONES:CB_BONES + 128] = (p[:, None] // 64 == p[None, :] // 64)
    cb[:, CB_ONES:CB_ONES + 128] = 1.0
    R = np.zeros((128, 128), np.float32)
    for blk in range(2):
        o = blk * 64
        for i in range(16):
            R[o + 16 + i, o + i] = -1.0
            R[o + i, o + 16 + i] = 1.0
            R[o + 48 + i, o + 32 + i] = -1.0
            R[o + 32 + i, o + 48 + i] = 1.0
    cb[:, CB_ROPE:CB_ROPE + 128] = R
    cb[:, CB_IDENT:CB_IDENT + 128] = np.eye(128)
    for mi in range(6):
        m = np.zeros((128, 512), np.float32)
        jrel = mi - 1
        for qb in range(4):
            ql = np.arange(128)[None, :]
            kl = np.arange(128)[:, None]
            if qb == jrel:
                blk = np.ones((128, 128))
            elif qb == jrel + 1:
                blk = (kl >= ql)
            elif qb == jrel - 1:
                blk = (kl <= ql)
            else:
                blk = np.zeros((128, 128))
            m[:, qb * 128:(qb + 1) * 128] = blk
        cb[:, CB_MASK + mi * 512: CB_MASK + (mi + 1) * 512] = m
    cf = np.zeros((128, NCF), np.float32)
    cf[:, CF_IDENT:CF_IDENT + 128] = np.eye(128)
    n = 2048
    rows = n // 64
    row = np.repeat(np.arange(rows, dtype=np.float32), 64)
    col = np.tile(np.arange(64, dtype=np.float32), rows)
    inv = (np.float32(10000.0) ** (-np.arange(0, 32, 2, dtype=np.float32) / np.float32(32))).astype(np.float32)
    ang_r = row[:, None] * inv[None, :]
    ang_c = col[:, None] * inv[None, :]
    ang = np.concatenate([ang_r, ang_r, ang_c, ang_c], axis=-1).astype(np.float32)
    cos = np.cos(ang).astype(np.float32).T
    sin = np.sin(ang).astype(np.float32).T
    cs = np.zeros((128, 4096), np.float32)
    cs[:, 0:2048] = np.concatenate([cos, cos], axis=0)
    cs[:, 2048:4096] = np.concatenate([sin, sin], axis=0)
    return cb, cf, cs


_NC_CACHE = {}


def kernel(x_prompt, x_sample, cache_even_k, cache_even_v, cache_odd_k, cache_odd_v, c, c_ctx,
           w_mod, b_mod, norm_g, ffn_w13, ffn_w2,
           even_w_in, even_w_out, even_qk_norm, even_lambda, even_subln, even_conv_w,
           odd_w_in, odd_w_out, odd_qk_norm, odd_sink, _cores=8):
    f = lambda a: np.ascontiguousarray(np.asarray(a, dtype=np.float32))
    x_prompt, x_sample = f(x_prompt), f(x_sample)
    cache_even_k, cache_even_v, cache_odd_k, cache_odd_v = map(f, (cache_even_k, cache_even_v, cache_odd_k, cache_odd_v))
    c, c_ctx, w_mod, b_mod, norm_g, ffn_w13, ffn_w2 = map(f, (c, c_ctx, w_mod, b_mod, norm_g, ffn_w13, ffn_w2))
    even_w_in, even_w_out, even_qk_norm, even_lambda, even_subln, even_conv_w = map(
        f, (even_w_in, even_w_out, even_qk_norm, even_lambda, even_subln, even_conv_w))
    odd_w_in, odd_w_out, odd_qk_norm, odd_sink = map(f, (odd_w_in, odd_w_out, odd_qk_norm, odd_sink))

    cb, cf, cs = _consts()
    if "nc" not in _NC_CACHE:
        _NC_CACHE["nc"] = build_nc()
    nc = _NC_CACHE["nc"]

    shared = {
        "cb": cb, "cf": cf, "cs": cs,
        "wmod0": np.ascontiguousarray(w_mod[0].reshape(8, 128, 18, 512).transpose(2, 1, 0, 3)).reshape(18, 128, 4096),
        "wmod1": np.ascontiguousarray(w_mod[1].reshape(8, 128, 72, 128).transpose(2, 1, 0, 3)).reshape(72, 128, 1024),
        "ewin": even_w_in[0], "ewout": even_w_out[0], "owin": odd_w_in[0], "owout": odd_w_out[0],
    }
    for l in range(2):
        for i in range(2):
            shared[f"w13_{l * 2 + i}"] = ffn_w13[l, i]
            shared[f"w2_{l * 2 + i}"] = ffn_w2[l, i]

    p = np.arange(128)
    in_maps = []
    for core in range(_cores):
        sp = np.zeros((128, NSP), np.float32)
        cvec = np.stack([c[core], c_ctx], axis=0)
        sp[:, SP_CT:SP_CT + 16] = cvec.reshape(2, 8, 128).transpose(2, 1, 0).reshape(128, 16)
        sp[:, SP_BT:SP_BT + 144] = b_mod.reshape(2, 72, 128).transpose(2, 0, 1).reshape(128, 144)
        sp[:, SP_GT:SP_GT + 48] = norm_g.reshape(6, 8, 128).transpose(2, 0, 1).reshape(128, 48)
        sp[:, SP_CW:SP_CW + 12] = even_conv_w[0].reshape(3, 4, 128).transpose(2, 1, 0).reshape(128, 12)
        sp[:, SP_QKG + 0] = even_qk_norm[0, 0][p % 64]
        sp[:, SP_QKG + 1] = even_qk_norm[0, 1][p % 64]
        sp[:, SP_QKG + 2] = odd_qk_norm[0, 0][p % 64]
        sp[:, SP_QKG + 3] = odd_qk_norm[0, 1][p % 64]
        sp[:, SP_SUB] = even_subln[0]
        sp[:, SP_LAM:SP_LAM + 256] = even_lambda[0].reshape(1, 256)
        sp[:, SP_SINK:SP_SINK + 16] = odd_sink[0].reshape(1, 16)
        sp[0:64, SP_SINKP:SP_SINKP + 8] = odd_sink[0][0::2].reshape(1, 8)
        sp[64:128, SP_SINKP:SP_SINKP + 8] = odd_sink[0][1::2].reshape(1, 8)
        m = dict(shared)
        m["xin"] = np.concatenate([x_sample[core], x_prompt[2 * core], x_prompt[2 * core + 1]], axis=0)
        m["cek"] = cache_even_k[core, 0].reshape(512, 512)
        m["cev"] = cache_even_v[core, 0].reshape(512, 512)
        m["cok"] = cache_odd_k[core, 0].reshape(512, 256)
        m["cov"] = cache_odd_v[core, 0].reshape(512, 256)
        m["spar"] = sp
        in_maps.append(m)

    res = run_bass_kernel_spmd(nc, in_maps, core_ids=list(range(_cores)))
    R = res.results
    nb = 2 * _cores
    y_prompt = np.zeros((nb, 256, D), np.float32)
    y_sample = np.zeros((_cores, 2048, D), np.float32)
    nek = np.zeros((nb, 1, 256, 4, 128), np.float32)
    nev = np.zeros((nb, 1, 256, 4, 128), np.float32)
    nok = np.zeros((nb, 1, 256, 4, 64), np.float32)
    nov = np.zeros((nb, 1, 256, 4, 64), np.float32)
    for core in range(_cores):
        r = R[core]
        y_sample[core] = r["ys"]
        y_prompt[2 * core:2 * core + 2] = r["yp"].reshape(2, 256, D)
        nek[2 * core:2 * core + 2, 0] = r["nek"].reshape(2, 256, 4, 128)
        nev[2 * core:2 * core + 2, 0] = r["nev"].reshape(2, 256, 4, 128)
        nok[2 * core:2 * core + 2, 0] = r["nok"].reshape(2, 256, 4, 64)
        nov[2 * core:2 * core + 2, 0] = r["nov"].reshape(2, 256, 4, 64)
    return (y_prompt, y_sample, nek, nev, nok, nov)
```

```python
import numpy as np
from contextlib import ExitStack
import concourse.bass as bass
import concourse.mybir as mybir
from concourse.bass_utils import run_bass_kernel_spmd

F32 = mybir.dt.float32
BF16 = mybir.dt.bfloat16
ALU = mybir.AluOpType
AF = mybir.ActivationFunctionType
AX = mybir.AxisListType

ENGS = ("pe", "act", "dve", "pool", "sp")

D = 1024
NT = 2560
TL = 512
NTILE = 5
NBLK = 20
DFF = 2816
NF = 22
G = 2
NFG = NF // G
EPS = 1e-6
SLOT = 6144
NF32 = 8
NBF = 8


class Sched:
    def __init__(self):
        self.streams = {e: [] for e in ENGS}
        self.cnt = {}
        self.res = {}
        self.seen = {e: {} for e in ENGS}
        self.sem_names = []

    def _sem(self, name):
        if name not in self.cnt:
            self.cnt[name] = 0
            self.sem_names.append(name)
        return name

    def op(self, eng, fn, reads=(), writes=(), dma=None, ndma=1):
        own = self._sem("E_" + eng)
        need = {}

        def want(evt, is_reader):
            if evt is None:
                return
            s, v = evt
            if s == own and dma is None:
                if eng == "pe":
                    return
            if need.get(s, 0) < v:
                need[s] = v

        for k in reads:
            r = self.res.get(k)
            if r is not None:
                want(r[0], False)
                if isinstance(k, tuple) and k[0] == "ps":
                    for s, v in r[1].items():
                        if s != own:
                            want((s, v), True)
        for k in writes:
            r = self.res.get(k)
            if r is not None:
                want(r[0], False)
                for s, v in r[1].items():
                    want((s, v), True)
        waits = []
        seen = self.seen[eng]
        for s, v in need.items():
            if seen.get(s, 0) < v:
                seen[s] = v
                waits.append((s, v))
        if dma is None:
            self.cnt[own] += 1
            evt = (own, self.cnt[own])
            self.streams[eng].append((waits, fn, ("c", own, 1)))
        else:
            self._sem(dma)
            self.cnt[dma] += 16 * ndma
            evt = (dma, self.cnt[dma])
            self.streams[eng].append((waits, fn, ("d", dma, ndma)))
        for k in reads:
            r = self.res.setdefault(k, [None, {}])
            if r[1].get(evt[0], 0) < evt[1]:
                r[1][evt[0]] = evt[1]
        for k in writes:
            self.res[k] = [evt, {}]
        return evt

    def emit(self, nc, st):
        H = {}
        for name in self.sem_names:
            H[name] = st.enter_context(nc.semaphore(name))
        block = st.enter_context(nc.Block())
        final = [(s, v) for s, v in self.cnt.items() if v > 0]

        def make(eng):
            stream = self.streams[eng]

            def body(e):
                for waits, fn, inc in stream:
                    for s, v in waits:
                        e.wait_ge(H[s], v)
                    r = fn(e)
                    if inc[0] == "c":
                        r.then_inc(H[inc[1]], 1)
                    else:
                        assert len(r) == inc[2], (len(r), inc[2])
                        for ins in r:
                            ins.then_inc(H[inc[1]], 16)
                if eng == "sp":
                    for s, v in final:
                        e.wait_ge(H[s], v)
            return body

        block.tensor(make("pe"))
        block.scalar(make("act"))
        block.vector(make("dve"))
        block.gpsimd(make("pool"))
        block.sync(make("sp"))


class TPool:
    def __init__(self, tiles, prefix):
        self.tiles = tiles
        self.prefix = prefix
        self.i = 0

    def get(self):
        n = len(self.tiles)
        j = self.i % n
        self.i += 1
        return self.tiles[j], (self.prefix, j)


SP_CT = 0
SP_BT = SP_CT + 16
SP_GT = SP_BT + 144
SP_CW = SP_GT + 48
SP_QKG = SP_CW + 12
SP_SUB = SP_QKG + 4
SP_LAM = SP_SUB + 1
SP_SINK = SP_LAM + 256
SP_SINKP = SP_SINK + 16
NSP = SP_SINKP + 8

CB_BONES = 0
CB_ONES = 128
CB_ROPE = 256
CB_IDENT = 384
CB_MASK = 512
NCB = CB_MASK + 6 * 512

CF_IDENT = 0
NCF = 128


def build_nc():
    nc = bass.Bass("TRN2", target_bir_lowering=False)
    S = Sched()

    def din(name, shape):
        return nc.dram_tensor(name, list(shape), F32, kind="ExternalInput").ap()

    def dout(name, shape):
        return nc.dram_tensor(name, list(shape), F32, kind="ExternalOutput").ap()

    xin = din("xin", [NT, D])
    cek = din("cek", [512, 512])
    cev = din("cev", [512, 512])
    cok = din("cok", [512, 256])
    cov = din("cov", [512, 256])
    spar = din("spar", [128, NSP])
    cb_d = din("cb", [128, NCB])
    cf_d = din("cf", [128, NCF])
    cs_d = din("cs", [128, 4096])
    wmod0 = din("wmod0", [18, 128, 4096])
    wmod1 = din("wmod1", [72, 128, 1024])
    w13 = [din(f"w13_{k}", [D, 2 * DFF]) for k in range(4)]
    w2 = [din(f"w2_{k}", [DFF, D]) for k in range(4)]
    ewin = din("ewin", [D, 3072])
    ewout = din("ewout", [D, D])
    owin = din("owin", [D, 1536])
    owout = din("owout", [D, D])

    ys = dout("ys", [2048, D])
    yp = dout("yp", [512, D])
    nek = dout("nek", [512, 512])
    nev = dout("nev", [512, 512])
    nok = dout("nok", [512, 256])
    nov = dout("nov", [512, 256])

    st = ExitStack()

    def sbt(name, shape, dt):
        return st.enter_context(nc.sbuf_tensor(name, list(shape), dt))

    xT = sbt("xT", [128, 8, NT], F32)
    hT = sbt("hT", [128, 8, NT], BF16)
    wslot = [sbt(f"wslot{i}", [128, SLOT], BF16) for i in range(2)]
    spt = sbt("spt", [128, NSP], F32)
    cbt = sbt("cbt", [128, NCB], BF16)
    cft = sbt("cft", [128, NCF], F32)
    mT = sbt("mT", [128, 2, 72, 2], F32)
    AB = sbt("AB", [128, 2, 9, 8, 2], F32)
    scT = sbt("scT", [128, 8, 2], BF16)
    misc = sbt("misc", [128, 64], F32)
    big = sbt("big", [128, 7680], BF16)
    kT = big[:, 0:3072]
    v3 = big[:, 3072:7680].rearrange("p (b f) -> p b f", b=24)
    vsb = v3[:, :, 0:128]
    ubuf = big[:, 0:5128].bitcast(F32)
    mslot = [big[:, 5632 + i * 1024: 5632 + (i + 1) * 1024] for i in range(2)]
    onp = TPool([sbt(f"on_{i}", [128, TL], BF16) for i in range(2)], "o")
    actb = [[sbt(f"act{p}_{j}", [128, TL], BF16) for j in range(G)] for p in range(2)]
    qtp = TPool([sbt(f"qT_{i}", [128, TL], BF16) for i in range(2)], "q")
    rsp = TPool([sbt(f"rs_{i}", [128, TL], F32) for i in range(1)], "r")
    f32p = TPool([sbt(f"f32_{i}", [128, TL], F32) for i in range(NF32)], "f")
    bfp = TPool([sbt(f"bf_{i}", [128, TL], BF16) for i in range(NBF)], "b")
    ps = [st.enter_context(nc.psum_tensor(f"ps{i}", [128, TL], F32)) for i in range(8)]

    def PK(i):
        return ("ps", i)

    bones = cbt[:, CB_BONES:CB_BONES + 128]
    ones = cbt[:, CB_ONES:CB_ONES + 128]
    ropeR = cbt[:, CB_ROPE:CB_ROPE + 128]
    identb = cbt[:, CB_IDENT:CB_IDENT + 128]
    identf = cft[:, CF_IDENT:CF_IDENT + 128]

    def maskap(mi, n=TL):
        return cbt[:, CB_MASK + mi * 512: CB_MASK + mi * 512 + n]


    def mm(out, pairs, reads, writes, start=True, stop=True):
        def fn(e, out=out, pairs=pairs, start=start, stop=stop):
            n = len(pairs)
            r = None
            for i, (l, rr) in enumerate(pairs):
                r = e.matmul(out, lhsT=l, rhs=rr, start=(start and i == 0), stop=(stop and i == n - 1))
            return r
        S.op("pe", fn, reads=reads, writes=writes)

    def tr(out, in_, ident, reads, writes):
        S.op("pe", lambda e, out=out, in_=in_, ident=ident: e.transpose(out, in_, ident),
             reads=reads, writes=writes)

    def act(out, in_, func, reads, writes, bias=None, scale=None):
        kw = {}
        if bias is not None:
            kw["bias"] = bias
        if scale is not None:
            kw["scale"] = scale
        S.op("act", lambda e, out=out, in_=in_, func=func, kw=kw: e.activation(out=out, in_=in_, func=func, **kw),
             reads=reads, writes=writes)

    def tt(eng, out, in0, in1, op, reads, writes):
        S.op(eng, lambda e, out=out, in0=in0, in1=in1, op=op: e.tensor_tensor(out=out, in0=in0, in1=in1, op=op),
             reads=reads, writes=writes)

    def stt(eng, out, in0, scalar, in1, op0, op1, reads, writes):
        S.op(eng, lambda e, out=out, in0=in0, scalar=scalar, in1=in1, op0=op0, op1=op1:
             e.scalar_tensor_tensor(out=out, in0=in0, scalar=scalar, in1=in1, op0=op0, op1=op1),
             reads=reads, writes=writes)

    def ts(eng, out, in0, s1, s2, op0, op1, reads, writes):
        if s2 is None:
            S.op(eng, lambda e, out=out, in0=in0, s1=s1, op0=op0:
                 e.tensor_scalar(out=out, in0=in0, scalar1=s1, scalar2=None, op0=op0),
                 reads=reads, writes=writes)
        else:
            S.op(eng, lambda e, out=out, in0=in0, s1=s1, s2=s2, op0=op0, op1=op1:
                 e.tensor_scalar(out=out, in0=in0, scalar1=s1, scalar2=s2, op0=op0, op1=op1),
                 reads=reads, writes=writes)

    def cp(eng, out, in_, reads, writes):
        if eng == "act":
            S.op("act", lambda e, out=out, in_=in_: e.copy(out=out, in_=in_), reads=reads, writes=writes)
        else:
            S.op(eng, lambda e, out=out, in_=in_: e.tensor_copy(out=out, in_=in_), reads=reads, writes=writes)

    def recip(out, in_, reads, writes):
        S.op("dve", lambda e, out=out, in_=in_: e.reciprocal(out=out, in_=in_), reads=reads, writes=writes)

    def dma(eng, pairs, sem, reads, writes):
        def fn(e, pairs=pairs):
            return [e.dma_start(out=o, in_=i) for o, i in pairs]
        S.op(eng, fn, reads=reads, writes=writes, dma=sem, ndma=len(pairs))

    rr = {"i": 0}

    def alt():
        rr["i"] += 1
        return "act" if rr["i"] % 2 else "dve"

    def cond_of(t):
        return 1 if t == 4 else 0

    def xk(c, t):
        return ("x", c, t)

    def hk(c, t):
        return ("h", c, t)

    dma("sp", [(spt[:], spar), (cft[:], cf_d)], "D_const", [], ["spt", "cft"])
    dma("pool", [(cbt[:], cb_d)], "D_constb", [], ["cbt"])
    act(scT[:].rearrange("p c r -> p (c r)"), spt[:, SP_CT:SP_CT + 16], AF.Silu, ["spt"], ["scT"])

    units = []

    def WK(s):
        return ("w", s)

    def make_mod_unit(l, jg):
        def load(s):
            assert l == 0
            dma("pool", [(wslot[s][:, 0:4096], wmod0[jg])], f"D_w{s}", [], [WK(s)])

        def compute(s):
            ffn_flush()
            wv = wslot[s][:, 0:4096].rearrange("p (kc n) -> p kc n", kc=8)
            bank = 7
            for cc in range(4):
                idx = jg * 4 + cc
                mm(ps[bank][:, cc * 2:cc * 2 + 2],
                   [(wv[:, kc, cc * 128:(cc + 1) * 128], scT[:, kc, :]) for kc in range(8)],
                   [WK(s), "scT"], [PK(bank)])
            for r in range(2):
                src = ps[bank][:, 0:8].rearrange("p (a r) -> p a r", r=2)[:, :, r]
                tt("dve", mT[:, l, jg * 4:jg * 4 + 4, r], src,
                   spt[:, SP_BT + l * 72 + jg * 4: SP_BT + l * 72 + jg * 4 + 4], ALU.add,
                   [PK(bank), "spt"], [("mT", l)])
        return load, compute


    bg_tasks = []

    def make_small_mod_tasks(l):
        def load(k):
            sl = k % 2
            assert l == 1
            dma("pool", [(mslot[sl], wmod1[k])], f"D_m{sl}", ["BIG"], [("mw", sl)])

        def task(k):
            sl = k % 2
            wv = mslot[sl].rearrange("p (kc n) -> p kc n", kc=8)
            mm(ps[7][:, 8:10], [(wv[:, kc, :], scT[:, kc, :]) for kc in range(8)], [("mw", sl), "scT", "BIG"], [PK(7)])
            tt("dve", mT[:, l, k, :], ps[7][:, 8:10], spt[:, SP_BT + l * 72 + k: SP_BT + l * 72 + k + 1].to_broadcast([128, 2]),
               ALU.add, [PK(7), "spt"], [("mT", l)])
            if k + 2 < 72:
                load(k + 2)
        load(0)
        load(1)
        for k in range(72):
            bg_tasks.append(lambda k=k: task(k))

    def run_bg(n):
        for _ in range(n):
            if bg_tasks:
                bg_tasks.pop(0)()

    def derive_scalars(l):
        for i in range(3):
            g = spt[:, SP_GT + (l * 3 + i) * 8: SP_GT + (l * 3 + i) * 8 + 8]
            for r in range(2):
                sh = mT[:, l, (3 * i + 0) * 8:(3 * i + 0) * 8 + 8, r]
                sc = mT[:, l, (3 * i + 1) * 8:(3 * i + 1) * 8 + 8, r]
                gt = mT[:, l, (3 * i + 2) * 8:(3 * i + 2) * 8 + 8, r]
                stt("dve", AB[:, l, 3 * i + 0, :, r], sc, 1.0, g, ALU.add, ALU.mult,
                    [("mT", l), "spt"], [("AB", l)])
                cp("dve", AB[:, l, 3 * i + 1, :, r], sh, [("mT", l)], [("AB", l)])
                ts("dve", AB[:, l, 3 * i + 2, :, r], gt, 0.5 if i != 1 else 1.0, None, ALU.mult, None,
                   [("mT", l)], [("AB", l)])

    def ABs(l, kind, c, t):
        return AB[:, l, kind, c, cond_of(t):cond_of(t) + 1]

    def norm_stage(l, i):
        for t in range(NTILE):
            tsl = slice(t * TL, (t + 1) * TL)
            bank = 6
            for c in range(8):
                sq, sqk = bfp.get()
                act(sq[:], xT[:, c, tsl], AF.Square, [xk(c, t)], [sqk])
                mm(ps[bank][:], [(ones, sq[:])], [sqk, "cbt"], [PK(bank)], start=(c == 0), stop=(c == 7))
            rs, rsk = rsp.get()
            act(rs[:], ps[bank][:], AF.Ln, [PK(bank)], [rsk], bias=EPS, scale=1.0 / D)
            act(rs[:], rs[:], AF.Exp, [rsk], [rsk], scale=-0.5)
            for c in range(8):
                tmp, tk = f32p.get()
                stt("dve", tmp[:], xT[:, c, tsl], ABs(l, 3 * i, c, t), rs[:], ALU.mult, ALU.mult,
                    [xk(c, t), rsk, ("AB", l)], [tk])
                act(hT[:, c, tsl], tmp[:], AF.Identity, [tk, ("AB", l)], [hk(c, t)],
                    bias=ABs(l, 3 * i + 1, c, t), scale=1.0)

    ffn_state = {"jc": 0, "dcc": 0, "pending": None}
    att_state = {"c": 0}
    conv_pend = {"v": None}

    def make_ffn_unit(l, i, g):
        k = l * 2 + i
        f0 = g * G
        gk = 3 * (2 * i) + 2
        n13 = 8 * 2 * G * 128

        def load(s):
            wv = wslot[s][:, 0:n13].rearrange("p (kc gu n) -> p kc gu n", kc=8, gu=2)
            pairs = []
            for gu in range(2):
                src = w13[k][:, gu * DFF + f0 * 128: gu * DFF + (f0 + G) * 128].rearrange("(kc p) n -> p kc n", p=128)
                pairs.append((wv[:, :, gu, :], src))
            w2v = wslot[s][:, n13:n13 + G * 1024].rearrange("p (j d) -> p j d", j=G)
            pairs.append((w2v, w2[k][f0 * 128:(f0 + G) * 128, :].rearrange("(j p) d -> p j d", p=128)))
            dma("pool", pairs, f"D_w{s}", [], [WK(s)])

        def compute(s):
            wv = wslot[s][:, 0:n13].rearrange("p (kc gu n) -> p kc gu n", kc=8, gu=2)
            w2v = wslot[s][:, n13:n13 + G * 1024].rearrange("p (j d) -> p j d", j=G)
            for t in range(NTILE):
                tsl = slice(t * TL, (t + 1) * TL)
                par = (g * NTILE + t) % 2
                for j in range(G):
                    jc = ffn_state["jc"]
                    ffn_state["jc"] += 1
                    bg_, bu_ = (0, 1) if jc % 2 == 0 else (2, 3)
                    mm(ps[bg_][:], [(wv[:, kc, 0, j * 128:(j + 1) * 128], hT[:, kc, tsl]) for kc in range(8)],
                       [WK(s)] + [hk(c, t) for c in range(8)], [PK(bg_)])
                    mm(ps[bu_][:], [(wv[:, kc, 1, j * 128:(j + 1) * 128], hT[:, kc, tsl]) for kc in range(8)],
                       [WK(s)] + [hk(c, t) for c in range(8)], [PK(bu_)])
                    sg, sgk = f32p.get()
                    act(sg[:], ps[bg_][:], AF.Silu, [PK(bg_)], [sgk])
                    tt("dve", actb[par][j][:], sg[:], ps[bu_][:], ALU.mult, [sgk, PK(bu_)], [("act", par, j)])
                prev = ffn_state["pending"]
                if prev is not None:
                    prev()

                def w2step(t=t, tsl=tsl, par=par, s=s, w2v=w2v):
                    for dc in range(8):
                        dcc = ffn_state["dcc"]
                        ffn_state["dcc"] += 1
                        bk = 4 + dcc % 4
                        mm(ps[bk][:], [(w2v[:, j, dc * 128:(dc + 1) * 128], actb[par][j][:]) for j in range(G)],
                           [WK(s)] + [("act", par, j) for j in range(G)], [PK(bk)])
                        stt("dve", xT[:, dc, tsl], ps[bk][:], ABs(l, gk, dc, t), xT[:, dc, tsl], ALU.mult, ALU.add,
                            [PK(bk), xk(dc, t), ("AB", l)], [xk(dc, t)])
                ffn_state["pending"] = w2step
        return load, compute

    def ffn_flush():
        prev = ffn_state["pending"]
        if prev is not None:
            prev()
        ffn_state["pending"] = None

    def wout_accum(l, wrows, on, onk, t, wkey, banks=(2, 3)):
        tsl = slice(t * TL, (t + 1) * TL)
        for dc in range(8):
            bk = banks[dc % len(banks)]
            mm(ps[bk][:], [(wrows[:, dc * 128:(dc + 1) * 128], on)], [wkey, onk], [PK(bk)])
            stt("dve", xT[:, dc, tsl], ps[bk][:], ABs(l, 5, dc, t), xT[:, dc, tsl], ALU.mult, ALU.add,
                [PK(bk), xk(dc, t), ("AB", l)], [xk(dc, t)])

    def proj_norm_tasks(l, wcols, t, gcol, wkey, do_rope, dest, destk, banks=(0, 1, 1), out=None):
        pb, sb2, rb = banks
        tsl = slice(t * TL, (t + 1) * TL)
        stt_ = {}

        def t0():
            mm(ps[pb][:], [(wcols[:, kc, :], hT[:, kc, tsl]) for kc in range(8)],
               [wkey] + [hk(c, t) for c in range(8)], [PK(pb)])
            sq, sqk = bfp.get()
            act(sq[:], ps[pb][:], AF.Square, [PK(pb)], [sqk])
            stt_["sq"] = (sq, sqk)

        def t1():
            sq, sqk = stt_["sq"]
            mm(ps[sb2][:], [(bones, sq[:])], [sqk, "cbt"], [PK(sb2)])
            rs, rsk = f32p.get()
            act(rs[:], ps[sb2][:], AF.Ln, [PK(sb2)], [rsk], bias=EPS, scale=1.0 / 64)
            act(rs[:], rs[:], AF.Exp, [rsk], [rsk], scale=-0.5)
            kn, knk = f32p.get()
            stt("dve", kn[:], ps[pb][:], spt[:, SP_QKG + gcol:SP_QKG + gcol + 1], rs[:], ALU.mult, ALU.mult,
                [PK(pb), rsk, "spt"], [knk])
            stt_["kn"] = (kn, knk)
            if out is not None:
                out["kn"] = (kn, knk)
            if not do_rope:
                cp("act", dest, kn[:], [knk, "BIG"], [destk])
            else:
                knb, knbk = bfp.get()
                cp("act", knb[:], kn[:], [knk], [knbk])
                stt_["knb"] = (knb, knbk)
                cst, cstk = f32p.get()
                snt, sntk = f32p.get()
                dma("sp", [(cst[:], cs_d[:, t * TL:(t + 1) * TL])], "D_" + str(cstk), [], [cstk])
                dma("sp", [(snt[:], cs_d[:, 2048 + t * TL: 2048 + (t + 1) * TL])], "D_" + str(sntk), [], [sntk])
                stt_["cs"] = (cst, cstk, snt, sntk)

        def t2():
            kn, knk = stt_["kn"]
            knb, knbk = stt_["knb"]
            cst, cstk, snt, sntk = stt_["cs"]
            mm(ps[rb][:], [(ropeR, knb[:])], [knbk, "cbt"], [PK(rb)])
            tt("dve", cst[:], kn[:], cst[:], ALU.mult, [knk, cstk], [cstk])
            tt("dve", snt[:], ps[rb][:], snt[:], ALU.mult, [PK(rb), sntk], [sntk])
            tt("dve", dest, cst[:], snt[:], ALU.add, [cstk, sntk, "BIG"], [destk])

        return [t0, t1] + ([t2] if do_rope else [])

    def proj_norm(l, wcols, t, gcol, wkey, do_rope, dest=None, destk=None, banks=(0, 1, 1)):
        if dest is None:
            ob, obk = bfp.get()
            dest = ob[:]
            destk = obk
        out = {}
        for f in proj_norm_tasks(l, wcols, t, gcol, wkey, do_rope, dest, destk, banks=banks, out=out):
            f()
        kn, knk = out["kn"]
        return kn, knk, dest, destk


    def k_stage_pipelined(l, kcols, gcol, wkey, ctx_dst, col0, ncol):
        tl = []
        outs = []
        for t in range(NTILE):
            kd = kT[:, 512 + t * TL: 512 + (t + 1) * TL] if t < 4 else kT[:, 2560:3072]
            o = {}
            bk = (0, 1, 1) if t % 2 == 0 else (2, 3, 3)
            tl.append(proj_norm_tasks(l, kcols, t, gcol, wkey, t < 4, kd, ("kT", t), banks=bk, out=o))
            outs.append(o)
        for k in range(NTILE + 2):
            for d in range(3):
                t = k - d
                if 0 <= t < NTILE and d < len(tl[t]):
                    tl[t][d]()
                    if t == 4 and d == 1:
                        kn, knk = outs[4]["kn"]
                        store_ctx_k(kn, knk, ctx_dst, col0, ncol)

    def store_ctx_k(kn, knk, dst, col0, ncol):
        for blk in range(4):
            tr(ps[5][:, blk * 128:(blk + 1) * 128], kn[:, blk * 128:(blk + 1) * 128], identf, [knk, "cft"], [PK(5)])
        stg, stgk = f32p.get()
        cp("act", stg[:], ps[5][:], [PK(5)], [stgk])
        src = stg[:].rearrange("p (b f) -> p b f", b=4)[:, :, 0:ncol]
        d = dst[:, col0:col0 + ncol].rearrange("(b p) f -> p b f", p=128)
        dma("sp", [(d, src)], "D_" + str(stgk), [stgk], [])

    def load_cache_kT(src_ap, ndup, sem):
        stg, stgk = f32p.get()
        sv = stg[:].rearrange("p (b f) -> p b f", b=4)
        pairs = []
        w = 128 // ndup
        for dd in range(ndup):
            pairs.append((sv[:, :, dd * w:(dd + 1) * w], src_ap.rearrange("(b p) f -> p b f", p=128)))
        dma("sp", pairs, "D_" + str(stgk), [], [stgk])
        for blk in range(4):
            tr(ps[4][:, blk * 128:(blk + 1) * 128], stg[:, blk * 128:(blk + 1) * 128], identf, [stgk, "cft"], [PK(4)])
        cp("dve", kT[:, 0:512], ps[4][:], [PK(4), "BIG"], ["kT_cache"])

    def v_stage(wv_cols, wkey, ctx_dst, col0, ncol, odd=False):
        for bg4 in range(5):
            bank = 2 + bg4 % 4
            for b4 in range(4):
                blk = bg4 * 4 + b4
                mm(ps[bank][:, b4 * 128:(b4 + 1) * 128],
                   [(hT[:, kc, blk * 128:(blk + 1) * 128], wv_cols[:, kc, :]) for kc in range(8)],
                   [wkey] + [hk(c, bg4) for c in range(8)], [PK(bank)])
            pv = ps[bank][:].rearrange("p (b f) -> p b f", b=4)
            b0 = 4 + bg4 * 4
            if not odd:
                cp(alt(), v3[:, b0:b0 + 4, 0:128], pv, [PK(bank), "BIG"], [("v", bg4)])
            else:
                cp("act", v3[:, b0:b0 + 4, 0:64], pv[:, :, 0:64], [PK(bank), "BIG"], [("v", bg4)])
                cp("dve", v3[:, b0:b0 + 4, 128:192], pv[:, :, 64:128], [PK(bank), "BIG", ("v", bg4)], [("v", bg4)])
            if bg4 == 4:
                stg, stgk = f32p.get()
                cp("act", stg[:], ps[bank][:], [PK(bank)], [stgk])
                src = stg[:].rearrange("p (b f) -> p b f", b=4)[:, :, 0:ncol]
                d = ctx_dst[:, col0:col0 + ncol].rearrange("(b p) f -> p b f", p=128)
                dma("sp", [(d, src)], "D_" + str(stgk), [stgk], [])

    L0 = 0

    def make_even_attn_unit(h):
        def load(s):
            wv = wslot[s][:, 0:3072].rearrange("p (kc a n) -> p kc a n", kc=8, a=3)
            pairs = []
            for a in range(3):
                src = ewin[:, a * 512 + h * 128: a * 512 + (h + 1) * 128].rearrange("(kc p) n -> p kc n", p=128)
                pairs.append((wv[:, :, a, :], src))
            pairs.append((wslot[s][:, 3072:4096], ewout[h * 128:(h + 1) * 128, :]))
            dma("pool", pairs, f"D_w{s}", [], [WK(s)])

        def compute(s):
            l = L0
            wv = wslot[s][:, 0:3072].rearrange("p (kc a n) -> p kc a n", kc=8, a=3)
            wrows = wslot[s][:, 3072:4096]
            load_cache_kT(cek[:, h * 128:(h + 1) * 128], 1, None)
            dma("pool", [(v3[:, 0:4, 0:128], cev[:, h * 128:(h + 1) * 128].rearrange("(b p) f -> p b f", p=128))],
                "D_vc", ["BIG"], [("v", "c")])
            k_stage_pipelined(l, wv[:, :, 1, :], 1, WK(s), nek, h * 128, 128)
            import os
            _dbg = int(os.environ.get("KDBG", "99"))
            if h >= 1 and _dbg <= 1:
                return
            v_stage(wv[:, :, 2, :], WK(s), nev, h * 128, 128)
            if h >= 1 and _dbg <= 2:
                return
            def qchain(t_):
                qt_, qtk_ = qtp.get()
                return (proj_norm_tasks(l, wv[:, :, 0, :], t_, 0, WK(s), t_ < 4, qt_[:], qtk_, banks=(0, 1, 1)), qt_[:], qtk_)

            tasks, qT, qTk = qchain(0)
            for f in tasks:
                f()
            for t in range(NTILE):
                if t < 4:
                    segs = [(0, TL, [(kc * 128, kc, ["kT_cache", ("v", "c")] if kc < 4 else
                                      [("kT", (kc - 4) // 4), ("v", (kc - 4) // 4)]) for kc in range(20)])]
                else:
                    segs = [(sq_ * 256, 256, [(2560 + sq_ * 256 + k2 * 128, 20 + sq_ * 2 + k2, [("kT", 4), ("v", 4)])
                                              for k2 in range(2)]) for sq_ in range(2)]
                for q0, qn, chunks in segs:
                    nck = len(chunks)
                    pend = None
                    for ci in range(nck + 1):
                        cur = None
                        if ci < nck:
                            kc0, vb, rkeys = chunks[ci]
                            sb = (att_state["c"] % 2) * 2
                            att_state["c"] += 1
                            mm(ps[sb][:, 0:qn], [(kT[0:64, kc0:kc0 + 128], qT[0:64, q0:q0 + qn])], [qTk] + rkeys, [PK(sb)])
                            mm(ps[sb + 1][:, 0:qn], [(kT[64:128, kc0:kc0 + 128], qT[64:128, q0:q0 + qn])], [qTk] + rkeys, [PK(sb + 1)])
                            p1, p1k = bfp.get()
                            act(p1[:, 0:qn], ps[sb][:, 0:qn], AF.Exp, [PK(sb)], [p1k], scale=0.125)
                            p2, p2k = bfp.get()
                            act(p2[:, 0:qn], ps[sb + 1][:, 0:qn], AF.Exp, [PK(sb + 1)], [p2k], scale=0.125)
                            cur = (p1, p1k, p2, p2k, vb, rkeys, ci)
                        if pend is not None:
                            p1, p1k, p2, p2k, vb, rkeys, cj = pend
                            st_, sp_ = (cj == 0), (cj == nck - 1)
                            mm(ps[4][:, q0:q0 + qn], [(vsb[:, vb, :], p1[:, 0:qn])], [p1k] + rkeys, [PK(4)], st_, sp_)
                            mm(ps[6][:, q0:q0 + qn], [(ones, p1[:, 0:qn])], [p1k, "cbt"], [PK(6)], st_, sp_)
                            mm(ps[5][:, q0:q0 + qn], [(vsb[:, vb, :], p2[:, 0:qn])], [p2k] + rkeys, [PK(5)], st_, sp_)
                            mm(ps[7][:, q0:q0 + qn], [(ones, p2[:, 0:qn])], [p2k, "cbt"], [PK(7)], st_, sp_)
                        pend = cur
                ntasks = []
                if t + 1 < NTILE:
                    ntasks, nqT, nqTk = qchain(t + 1)
                if ntasks:
                    ntasks.pop(0)()
                r1, r1k = f32p.get()
                recip(r1[:], ps[6][:], [PK(6)], [r1k])
                a1, a1k = f32p.get()
                tt("dve", a1[:], ps[4][:], r1[:], ALU.mult, [PK(4), r1k], [a1k])
                r2, r2k = f32p.get()
                recip(r2[:], ps[7][:], [PK(7)], [r2k])
                a2, a2k = f32p.get()
                tt("dve", a2[:], ps[5][:], r2[:], ALU.mult, [PK(5), r2k], [a2k])
                stt("dve", a1[:], a2[:], misc[:, 0:1], a1[:], ALU.mult, ALU.add, [a2k, a1k, "misc"], [a1k])
                if ntasks:
                    ntasks.pop(0)()
                sq, sqk = bfp.get()
                act(sq[:], a1[:], AF.Square, [a1k], [sqk])
                mm(ps[2][:], [(ones, sq[:])], [sqk, "cbt"], [PK(2)])
                rs, rsk = f32p.get()
                act(rs[:], ps[2][:], AF.Ln, [PK(2)], [rsk], bias=EPS, scale=1.0 / 128)
                act(rs[:], rs[:], AF.Exp, [rsk], [rsk], scale=-0.5)
                on, onk = bfp.get()
                stt("dve", on[:], a1[:], misc[:, 1:2], rs[:], ALU.mult, ALU.mult, [a1k, rsk, "misc"], [onk])
                while ntasks:
                    ntasks.pop(0)()
                wout_accum(l, wrows, on[:], onk, t, WK(s), banks=(4, 5, 6, 7))
                if t + 1 < NTILE:
                    qT, qTk = nqT, nqTk
        return load, compute

    def useg(t):
        if t < 4:
            return [(0, TL, t * TL + 1)]
        return [(0, 256, 2048 + 2), (256, 256, 2304 + 3)]

    def make_even_conv_unit(j):
        def load(s):
            wv = wslot[s][:, 0:3072].rearrange("p (kc a n) -> p kc a n", kc=8, a=3)
            pairs = []
            for a in range(3):
                src = ewin[:, 1536 + a * 512 + j * 128: 1536 + a * 512 + (j + 1) * 128].rearrange("(kc p) n -> p kc n", p=128)
                pairs.append((wv[:, :, a, :], src))
            pairs.append((wslot[s][:, 3072:4096], ewout[512 + j * 128: 512 + (j + 1) * 128, :]))
            dma("pool", pairs, f"D_w{s}", [], [WK(s)])

        def compute(s):
            l = L0
            wv = wslot[s][:, 0:3072].rearrange("p (kc a n) -> p kc a n", kc=8, a=3)
            wrows = wslot[s][:, 3072:4096]
            for t in range(NTILE):
                tsl = slice(t * TL, (t + 1) * TL)
                rd = [WK(s)] + [hk(c, t) for c in range(8)]
                b0_ = (t % 2) * 2
                mm(ps[b0_][:], [(wv[:, kc, 1, :], hT[:, kc, tsl]) for kc in range(8)], rd, [PK(b0_)])
                mm(ps[b0_ + 1][:], [(wv[:, kc, 2, :], hT[:, kc, tsl]) for kc in range(8)], rd, [PK(b0_ + 1)])
                xs, xsk = f32p.get()
                cp("act", xs[:], ps[b0_ + 1][:], [PK(b0_ + 1)], [xsk])
                for (o0, n, uc) in useg(t):
                    tt("dve", ubuf[:, uc:uc + n], ps[b0_][:, o0:o0 + n], xs[:, o0:o0 + n], ALU.mult,
                       [PK(b0_), xsk, "upad", "BIG"], [("u", t)])
                run_bg(2)
            for t in range(NTILE):
                tsl = slice(t * TL, (t + 1) * TL)
                rd = [WK(s)] + [hk(c, t) for c in range(8)]
                bk = t % 3
                mm(ps[bk][:], [(wv[:, kc, 0, :], hT[:, kc, tsl]) for kc in range(8)], rd, [PK(bk)])
                acc, acck = f32p.get()
                uk = [("u", tt_) for tt_ in range(NTILE)] + ["upad", "BIG"]
                for (o0, n, uc) in useg(t):
                    ts("dve", acc[:, o0:o0 + n], ubuf[:, uc - 1:uc - 1 + n], spt[:, SP_CW + j * 3: SP_CW + j * 3 + 1], None,
                       ALU.mult, None, uk + ["spt"], [acck])
                    stt("dve", acc[:, o0:o0 + n], ubuf[:, uc:uc + n], spt[:, SP_CW + j * 3 + 1: SP_CW + j * 3 + 2],
                        acc[:, o0:o0 + n], ALU.mult, ALU.add, uk + [acck, "spt"], [acck])
                    stt("dve", acc[:, o0:o0 + n], ubuf[:, uc + 1:uc + 1 + n], spt[:, SP_CW + j * 3 + 2: SP_CW + j * 3 + 3],
                        acc[:, o0:o0 + n], ALU.mult, ALU.add, uk + [acck, "spt"], [acck])
                y, yk = bfp.get()
                tt("dve", y[:], ps[bk][:], acc[:], ALU.mult, [PK(bk), acck], [yk])
                if conv_pend["v"] is not None:
                    y_, yk_, t_ = conv_pend["v"]
                    wout_accum(l, wrows, y_[:], yk_, t_, WK(s), banks=(3, 4, 5, 6))
                conv_pend["v"] = (y, yk, t)
                run_bg(2)
            y_, yk_, t_ = conv_pend["v"]
            wout_accum(l, wrows, y_[:], yk_, t_, WK(s), banks=(3, 4, 5, 6))
            conv_pend["v"] = None
        return load, compute

    L1 = 1

    def make_odd_unit(g):
        def load(s):
            qv = wslot[s][:, 0:2048].rearrange("p (kc n) -> p kc n", kc=8)
            kv = wslot[s][:, 2048:3072].rearrange("p (kc n) -> p kc n", kc=8)
            vv = wslot[s][:, 3072:4096].rearrange("p (kc n) -> p kc n", kc=8)
            ov = wslot[s][:, 4096:6144].rearrange("p (j d) -> p j d", j=2)
            pairs = [(qv, owin[:, g * 256:(g + 1) * 256].rearrange("(kc p) n -> p kc n", p=128))]
            for dd in range(2):
                pairs.append((kv[:, :, dd * 64:(dd + 1) * 64],
                              owin[:, 1024 + g * 64: 1024 + (g + 1) * 64].rearrange("(kc p) n -> p kc n", p=128)))
                pairs.append((vv[:, :, dd * 64:(dd + 1) * 64],
                              owin[:, 1280 + g * 64: 1280 + (g + 1) * 64].rearrange("(kc p) n -> p kc n", p=128)))
            pairs.append((ov, owout[g * 256:(g + 1) * 256, :].rearrange("(j p) d -> p j d", p=128)))
            dma("pool", pairs, f"D_w{s}", [], [WK(s)])

        def compute(s):
            l = L1
            qv = wslot[s][:, 0:2048].rearrange("p (kc n) -> p kc n", kc=8)
            kv = wslot[s][:, 2048:3072].rearrange("p (kc n) -> p kc n", kc=8)
            vv = wslot[s][:, 3072:4096].rearrange("p (kc n) -> p kc n", kc=8)
            ov = wslot[s][:, 4096:6144].rearrange("p (j d) -> p j d", j=2)
            BK = (6, 7, 6)
            if g == 0:
                S.op("pool", lambda e: e.memset(v3[:, :, 64:128], 1.0), reads=["BIG"],
                     writes=[("v", i) for i in range(5)] + [("v", "c")])
            load_cache_kT(cok[:, g * 64:(g + 1) * 64], 2, None)
            vsrc = cov[:, g * 64:(g + 1) * 64].rearrange("(b p) f -> p b f", p=128)
            dma("pool", [(v3[:, 0:4, 0:64], vsrc), (v3[:, 0:4, 128:192], vsrc)], "D_vc", ["BIG"], [("v", "c")])
            k_stage_pipelined(l, kv, 3, WK(s), nok, g * 64, 64)
            v_stage(vv, WK(s), nov, g * 64, 64, odd=True)

            hooks = []

            def run_hooks(n):
                for _ in range(n):
                    if hooks:
                        hooks.pop(0)()

            def qchain(i):
                t_, pr_ = divmod(i, 2)
                qt_, qtk_ = qtp.get()
                tasks = proj_norm_tasks(l, qv[:, :, pr_ * 128:(pr_ + 1) * 128], t_, 2, WK(s), t_ < 4,
                                        qt_[:], qtk_, banks=BK)
                return tasks, qt_[:], qtk_

            wtog = {"v": 0}
            on_stash = {"v": None}

            def wout_task(t_, dc, ons):
                tsl = slice(t_ * TL, (t_ + 1) * TL)
                if any(getattr(h_, "is_chain", False) for h_ in hooks):
                    bk = 7
                else:
                    wtog["v"] ^= 1
                    bk = 6 + wtog["v"]
                mm(ps[bk][:], [(ov[:, p_, dc * 128:(dc + 1) * 128], o_) for p_, (o_, _) in enumerate(ons)],
                   [WK(s)] + [k_ for _, k_ in ons], [PK(bk)])
                stt("dve", xT[:, dc, tsl], ps[bk][:], ABs(l, 5, dc, t_), xT[:, dc, tsl], ALU.mult, ALU.add,
                    [PK(bk), xk(dc, t_), ("AB", l)], [xk(dc, t_)])

            tasks, qT, qTk = qchain(0)
            for f in tasks:
                f()
            NI = 2 * NTILE
            for i in range(NI):
                t, pr = divmod(i, 2)
                pairidx = g * 2 + pr
                if i + 1 < NI:
                    ntasks, nqT, nqTk = qchain(i + 1)
                    for f_ in ntasks:
                        f_.is_chain = True
                    old = list(hooks)
                    del hooks[:]
                    while ntasks or old:
                        if ntasks:
                            hooks.append(ntasks.pop(0))
                        if old:
                            hooks.append(old.pop(0))
                if t < 4:
                    chunks = [(kc * 128, kc, ["kT_cache", ("v", "c")], None) for kc in range(4)]
                    for jb in range(max(0, 4 * t - 1), min(15, 4 * t + 4) + 1):
                        chunks.append((512 + jb * 128, 4 + jb, [("kT", jb // 4), ("v", jb // 4)], jb - 4 * t + 1))
                    segs = [(0, TL, chunks)]
                else:
                    segs = [(sq_ * 256, 256, [(2560 + sq_ * 256 + k2 * 128, 20 + sq_ * 2 + k2, [("kT", 4), ("v", 4)], None)
                                              for k2 in range(2)]) for sq_ in range(2)]
                for q0, qn, chunks in segs:
                    nck = len(chunks)
                    pend = None
                    for ci in range(nck + 1):
                        cur = None
                        if ci < nck:
                            kc0, vb, rkeys, mi = chunks[ci]
                            sb = (att_state["c"] % 2) * 2
                            att_state["c"] += 1
                            if mi is None:
                                c0, c1 = q0, q0 + qn
                            else:
                                jrel = mi - 1
                                c0, c1 = max(0, jrel - 1) * 128, (min(3, jrel + 1) + 1) * 128
                            cw = c1 - c0
                            mm(ps[sb][:, 0:cw], [(kT[0:64, kc0:kc0 + 128], qT[0:64, c0:c1])], [qTk] + rkeys, [PK(sb)])
                            mm(ps[sb + 1][:, 0:cw], [(kT[64:128, kc0:kc0 + 128], qT[64:128, c0:c1])], [qTk] + rkeys, [PK(sb + 1)])
                            p1, p1k = bfp.get()
                            act(p1[:, 0:cw], ps[sb][:, 0:cw], AF.Exp, [PK(sb)], [p1k], scale=0.125)
                            p2, p2k = bfp.get()
                            act(p2[:, 0:cw], ps[sb + 1][:, 0:cw], AF.Exp, [PK(sb + 1)], [p2k], scale=0.125)
                            if mi is not None:
                                for qb in range(c0 // 128, c1 // 128):
                                    if qb == jrel + 1:
                                        tri = cbt[:, CB_MASK: CB_MASK + 128]
                                    elif qb == jrel - 1:
                                        tri = cbt[:, CB_MASK + 5 * 512 + 384: CB_MASK + 6 * 512]
                                    else:
                                        continue
                                    o_ = qb * 128 - c0
                                    tt("dve", p1[:, o_:o_ + 128], p1[:, o_:o_ + 128], tri, ALU.mult, [p1k, "cbt"], [p1k])
                                    tt("dve", p2[:, o_:o_ + 128], p2[:, o_:o_ + 128], tri, ALU.mult, [p2k, "cbt"], [p2k])
                            cur = (p1, p1k, p2, p2k, vb, rkeys, ci, c0, c1)
                        if pend is not None:
                            p1, p1k, p2, p2k, vb, rkeys, cj, c0, c1 = pend
                            cw = c1 - c0
                            st_, sp_ = (cj == 0), (cj == nck - 1)
                            mm(ps[4][:, c0:c1], [(v3[:, vb, 0:128], p1[:, 0:cw])], [p1k] + rkeys, [PK(4)], st_, sp_)
                            mm(ps[5][:, c0:c1], [(v3[:, vb, 64:192], p2[:, 0:cw])], [p2k] + rkeys, [PK(5)], st_, sp_)
                        pend = cur
                        run_hooks(2)
                run_hooks(1000)
                zc, zck = f32p.get()
                cp("dve", zc[0:64, :], ps[4][64:128, :], [PK(4)], [zck])
                cp("dve", zc[64:128, :], ps[5][0:64, :], [PK(5), zck], [zck])
                ts("dve", zc[:], zc[:], misc[:, 24 + pairidx:25 + pairidx], None, ALU.add, None, [zck, "misc"], [zck])
                recip(zc[:], zc[:], [zck], [zck])
                on, onk = onp.get()
                tt("dve", on[0:64, :], ps[4][0:64, :], zc[0:64, :], ALU.mult, [PK(4), zck], [onk])
                tt("dve", on[64:128, :], ps[5][64:128, :], zc[64:128, :], ALU.mult, [PK(5), zck, onk], [onk])
                if pr == 0:
                    on_stash["v"] = (on[:], onk)
                else:
                    ons = [on_stash["v"], (on[:], onk)]
                    for dc in range(8):
                        hooks.append(lambda t=t, dc=dc, ons=ons: wout_task(t, dc, ons))
                if i + 1 < NI:
                    qT, qTk = nqT, nqTk
            run_hooks(1000)
        return load, compute

    def plain(fn):
        return (None, lambda s, fn=fn: fn())

    def load_x():
        for b in range(NBLK):
            t = b // 4
            for half in range(2):
                stg, stgk = f32p.get()
                dma("sp", [(stg[:], xin[b * 128:(b + 1) * 128, half * 512:(half + 1) * 512])], "D_" + str(stgk), [], [stgk])
                bank = (b * 2 + half) % 4
                for j in range(4):
                    tr(ps[bank][:, j * 128:(j + 1) * 128], stg[:, j * 128:(j + 1) * 128], identf, [stgk, "cft"], [PK(bank)])
                cp(alt(), xT[:, half * 4:half * 4 + 4, b * 128:(b + 1) * 128],
                   ps[bank][:].rearrange("p (j n) -> p j n", j=4), [PK(bank)],
                   [xk(c, t) for c in range(half * 4, half * 4 + 4)])

    def store_y():
        for b in range(NBLK):
            t = b // 4
            for half in range(2):
                bank = (b * 2 + half) % 4
                for j in range(4):
                    c = half * 4 + j
                    tr(ps[bank][:, j * 128:(j + 1) * 128], xT[:, c, b * 128:(b + 1) * 128], identf,
                       [xk(c, t), "cft"], [PK(bank)])
                stg, stgk = f32p.get()
                cp(alt(), stg[:], ps[bank][:], [PK(bank)], [stgk])
                if b < 16:
                    d = ys[b * 128:(b + 1) * 128, half * 512:(half + 1) * 512]
                else:
                    d = yp[(b - 16) * 128:(b - 15) * 128, half * 512:(half + 1) * 512]
                dma("sp", [(d, stg[:])], "D_" + str(stgk), [stgk], [])

    def setup_misc():
        lam_init = 0.8 - 0.6 * float(np.exp(-0.3 * 0))
        lm = spt[:, SP_LAM:SP_LAM + 256]
        pr_, prk = f32p.get()
        tt("dve", pr_[:, 0:64], lm[:, 0:64], lm[:, 64:128], ALU.mult, ["spt"], [prk])
        tt("dve", pr_[:, 64:128], lm[:, 128:192], lm[:, 192:256], ALU.mult, ["spt", prk], [prk])
        S.op("dve", lambda e: e.reduce_sum(out=misc[:, 2:3], in_=pr_[:, 0:64], axis=AX.X), reads=[prk], writes=["m2"])
        S.op("dve", lambda e: e.reduce_sum(out=misc[:, 3:4], in_=pr_[:, 64:128], axis=AX.X), reads=[prk], writes=["m3"])
        act(misc[:, 4:6], misc[:, 2:4], AF.Exp, ["m2", "m3"], ["m45"])
        tt("dve", misc[:, 6:7], misc[:, 5:6], misc[:, 4:5], ALU.subtract, ["m45"], ["m6"])
        ts("dve", misc[:, 0:1], misc[:, 6:7], -lam_init, None, ALU.add, None, ["m6"], ["misc0"])
        ts("dve", misc[:, 1:2], spt[:, SP_SUB:SP_SUB + 1], 1.0 - lam_init, None, ALU.mult, None, ["spt"], ["misc1"])
        act(misc[:, 8:32], spt[:, SP_SINK:SP_SINK + 24], AF.Exp, ["spt"], ["misc8"])
        S.op("pool", lambda e: e.memset(ubuf[:, 0:1], 0.0), writes=["up0"])
        S.op("pool", lambda e: e.memset(ubuf[:, 2049:2050], 0.0), writes=["up1"])
        S.op("pool", lambda e: e.memset(ubuf[:, 2306:2307], 0.0), writes=["up2"])
        S.op("pool", lambda e: e.memset(ubuf[:, 2563:2564], 0.0), writes=["up3"])
        S.op("dve", lambda e: e.memset(misc[:, 34:35], 0.0),
             reads=["misc0", "misc1", "misc8", "up0", "up1", "up2", "up3"], writes=["misc", "upad", "BIG"])

    seq = []
    seq.append(plain(load_x))
    for jg in range(18):
        seq.append(make_mod_unit(0, jg))
    seq.append(plain(lambda: derive_scalars(0)))
    seq.append(plain(setup_misc))
    seq.append(plain(lambda: norm_stage(0, 0)))
    for g in range(NFG):
        seq.append(make_ffn_unit(0, 0, g))
    seq.append(plain(ffn_flush))
    seq.append(plain(lambda: norm_stage(0, 1)))
    seq.append(plain(lambda: make_small_mod_tasks(1)))
    for j in range(4):
        seq.append(make_even_conv_unit(j))
    seq.append(plain(lambda: run_bg(1000)))
    seq.append(plain(lambda: S.op("dve", lambda e: e.memset(misc[:, 35:36], 0.0), reads=[], writes=["BIG"])))
    for h in range(4):
        seq.append(make_even_attn_unit(h))
    seq.append(plain(lambda: norm_stage(0, 2)))
    for g in range(NFG):
        seq.append(make_ffn_unit(0, 1, g))
    seq.append(plain(ffn_flush))
    seq.append(plain(lambda: derive_scalars(1)))
    seq.append(plain(lambda: norm_stage(1, 0)))
    for g in range(NFG):
        seq.append(make_ffn_unit(1, 0, g))
    seq.append(plain(ffn_flush))
    seq.append(plain(lambda: norm_stage(1, 1)))
    for g in range(4):
        seq.append(make_odd_unit(g))
    seq.append(plain(lambda: norm_stage(1, 2)))
    for g in range(NFG):
        seq.append(make_ffn_unit(1, 1, g))
    seq.append(plain(ffn_flush))
    seq.append(plain(store_y))

    widx = [i for i, u in enumerate(seq) if u[0] is not None]
    slot_of = {i: n % 2 for n, i in enumerate(widx)}
    nxt = {widx[n]: widx[n + 1] for n in range(len(widx) - 1)}
    if widx:
        seq[widx[0]][0](slot_of[widx[0]])
    import os
    _lim = int(os.environ.get("KSEQ_LIMIT", "100000"))
    for i, (ld, comp) in enumerate(seq):
        if i >= _lim:
            break
        if ld is not None:
            comp(slot_of[i])
            if i in nxt:
                seq[nxt[i]][0](slot_of[nxt[i]])
        else:
            comp(None)
    for _ in range(int(os.environ.get("KPAD", "0"))):
        S.op("dve", lambda e: e.memset(misc[:, 40:41], 0.0), reads=[], writes=["pad"])
        S.op("act", lambda e: e.copy(out=misc[:, 42:43], in_=misc[:, 40:41]), reads=["pad"], writes=["pad2"])
    if os.environ.get("KSEQ_PRINT"):
        print("SEM COUNTS", S.cnt)
        print("OPS", {e: len(v) for e, v in S.streams.items()}, "WAITS", {e: sum(len(w[0]) for w in v) for e, v in S.streams.items()})

    S.emit(nc, st)
    st.close()
    return nc


def _consts():
    cb = np.zeros((128, NCB), np.float32)
    p = np.arange(128)
    cb[:, CB_BONES:CB_BONES + 128] = (p[:, None] // 64 == p[None, :] // 64)
    cb[:, CB_ONES:CB_ONES + 128] = 1.0
    R = np.zeros((128, 128), np.float32)
    for blk in range(2):
        o = blk * 64
        for i in range(16):
            R[o + 16 + i, o + i] = -1.0
            R[o + i, o + 16 + i] = 1.0
            R[o + 48 + i, o + 32 + i] = -1.0
            R[o + 32 + i, o + 48 + i] = 1.0
    cb[:, CB_ROPE:CB_ROPE + 128] = R
    cb[:, CB_IDENT:CB_IDENT + 128] = np.eye(128)
    for mi in range(6):
        m = np.zeros((128, 512), np.float32)
        jrel = mi - 1
        for qb in range(4):
            ql = np.arange(128)[None, :]
            kl = np.arange(128)[:, None]
            if qb == jrel:
                blk = np.ones((128, 128))
            elif qb == jrel + 1:
                blk = (kl >= ql)
            elif qb == jrel - 1:
                blk = (kl <= ql)
            else:
                blk = np.zeros((128, 128))
            m[:, qb * 128:(qb + 1) * 128] = blk
        cb[:, CB_MASK + mi * 512: CB_MASK + (mi + 1) * 512] = m
    cf = np.zeros((128, NCF), np.float32)
    cf[:, CF_IDENT:CF_IDENT + 128] = np.eye(128)
    n = 2048
    rows = n // 64
    row = np.repeat(np.arange(rows, dtype=np.float32), 64)
    col = np.tile(np.arange(64, dtype=np.float32), rows)
    inv = (np.float32(10000.0) ** (-np.arange(0, 32, 2, dtype=np.float32) / np.float32(32))).astype(np.float32)
    ang_r = row[:, None] * inv[None, :]
    ang_c = col[:, None] * inv[None, :]
    ang = np.concatenate([ang_r, ang_r, ang_c, ang_c], axis=-1).astype(np.float32)
    cos = np.cos(ang).astype(np.float32).T
    sin = np.sin(ang).astype(np.float32).T
    cs = np.zeros((128, 4096), np.float32)
    cs[:, 0:2048] = np.concatenate([cos, cos], axis=0)
    cs[:, 2048:4096] = np.concatenate([sin, sin], axis=0)
    return cb, cf, cs


_NC_CACHE = {}


def kernel(x_prompt, x_sample, cache_even_k, cache_even_v, cache_odd_k, cache_odd_v, c, c_ctx,
           w_mod, b_mod, norm_g, ffn_w13, ffn_w2,
           even_w_in, even_w_out, even_qk_norm, even_lambda, even_subln, even_conv_w,
           odd_w_in, odd_w_out, odd_qk_norm, odd_sink, _cores=8):
    f = lambda a: np.ascontiguousarray(np.asarray(a, dtype=np.float32))
    x_prompt, x_sample = f(x_prompt), f(x_sample)
    cache_even_k, cache_even_v, cache_odd_k, cache_odd_v = map(f, (cache_even_k, cache_even_v, cache_odd_k, cache_odd_v))
    c, c_ctx, w_mod, b_mod, norm_g, ffn_w13, ffn_w2 = map(f, (c, c_ctx, w_mod, b_mod, norm_g, ffn_w13, ffn_w2))
    even_w_in, even_w_out, even_qk_norm, even_lambda, even_subln, even_conv_w = map(
        f, (even_w_in, even_w_out, even_qk_norm, even_lambda, even_subln, even_conv_w))
    odd_w_in, odd_w_out, odd_qk_norm, odd_sink = map(f, (odd_w_in, odd_w_out, odd_qk_norm, odd_sink))

    cb, cf, cs = _consts()
    if "nc" not in _NC_CACHE:
        _NC_CACHE["nc"] = build_nc()
    nc = _NC_CACHE["nc"]

    shared = {
        "cb": cb, "cf": cf, "cs": cs,
        "wmod0": np.ascontiguousarray(w_mod[0].reshape(8, 128, 18, 512).transpose(2, 1, 0, 3)).reshape(18, 128, 4096),
        "wmod1": np.ascontiguousarray(w_mod[1].reshape(8, 128, 72, 128).transpose(2, 1, 0, 3)).reshape(72, 128, 1024),
        "ewin": even_w_in[0], "ewout": even_w_out[0], "owin": odd_w_in[0], "owout": odd_w_out[0],
    }
    for l in range(2):
        for i in range(2):
            shared[f"w13_{l * 2 + i}"] = ffn_w13[l, i]
            shared[f"w2_{l * 2 + i}"] = ffn_w2[l, i]

    p = np.arange(128)
    in_maps = []
    for core in range(_cores):
        sp = np.zeros((128, NSP), np.float32)
        cvec = np.stack([c[core], c_ctx], axis=0)
        sp[:, SP_CT:SP_CT + 16] = cvec.reshape(2, 8, 128).transpose(2, 1, 0).reshape(128, 16)
        sp[:, SP_BT:SP_BT + 144] = b_mod.reshape(2, 72, 128).transpose(2, 0, 1).reshape(128, 144)
        sp[:, SP_GT:SP_GT + 48] = norm_g.reshape(6, 8, 128).transpose(2, 0, 1).reshape(128, 48)
        sp[:, SP_CW:SP_CW + 12] = even_conv_w[0].reshape(3, 4, 128).transpose(2, 1, 0).reshape(128, 12)
        sp[:, SP_QKG + 0] = even_qk_norm[0, 0][p % 64]
        sp[:, SP_QKG + 1] = even_qk_norm[0, 1][p % 64]
        sp[:, SP_QKG + 2] = odd_qk_norm[0, 0][p % 64]
        sp[:, SP_QKG + 3] = odd_qk_norm[0, 1][p % 64]
        sp[:, SP_SUB] = even_subln[0]
        sp[:, SP_LAM:SP_LAM + 256] = even_lambda[0].reshape(1, 256)
        sp[:, SP_SINK:SP_SINK + 16] = odd_sink[0].reshape(1, 16)
        sp[0:64, SP_SINKP:SP_SINKP + 8] = odd_sink[0][0::2].reshape(1, 8)
        sp[64:128, SP_SINKP:SP_SINKP + 8] = odd_sink[0][1::2].reshape(1, 8)
        m = dict(shared)
        m["xin"] = np.concatenate([x_sample[core], x_prompt[2 * core], x_prompt[2 * core + 1]], axis=0)
        m["cek"] = cache_even_k[core, 0].reshape(512, 512)
        m["cev"] = cache_even_v[core, 0].reshape(512, 512)
        m["cok"] = cache_odd_k[core, 0].reshape(512, 256)
        m["cov"] = cache_odd_v[core, 0].reshape(512, 256)
        m["spar"] = sp
        in_maps.append(m)

    res = run_bass_kernel_spmd(nc, in_maps, core_ids=list(range(_cores)))
    R = res.results
    nb = 2 * _cores
    y_prompt = np.zeros((nb, 256, D), np.float32)
    y_sample = np.zeros((_cores, 2048, D), np.float32)
    nek = np.zeros((nb, 1, 256, 4, 128), np.float32)
    nev = np.zeros((nb, 1, 256, 4, 128), np.float32)
    nok = np.zeros((nb, 1, 256, 4, 64), np.float32)
    nov = np.zeros((nb, 1, 256, 4, 64), np.float32)
    for core in range(_cores):
        r = R[core]
        y_sample[core] = r["ys"]
        y_prompt[2 * core:2 * core + 2] = r["yp"].reshape(2, 256, D)
        nek[2 * core:2 * core + 2, 0] = r["nek"].reshape(2, 256, 4, 128)
        nev[2 * core:2 * core + 2, 0] = r["nev"].reshape(2, 256, 4, 128)
        nok[2 * core:2 * core + 2, 0] = r["nok"].reshape(2, 256, 4, 64)
        nov[2 * core:2 * core + 2, 0] = r["nov"].reshape(2, 256, 4, 64)
    return (y_prompt, y_sample, nek, nev, nok, nov)
```
